# Optimizing a Trainium2 kernel written in Bass

```python
import math
import jax, jax.numpy as jnp
from jax import lax
import numpy as np

D_MODEL = 1024
BATCH = 2
SEQ = 16384
DEPTH = 1
DEC_BATCH = 32
DEC_SEQ = 32
PAST_LEN = 4096

CHUNK = 64
SSD_EXPAND = 2
SSD_INNER = SSD_EXPAND * D_MODEL
SSD_HEAD_DIM = 64
SSD_HEADS = SSD_INNER // SSD_HEAD_DIM
SSD_GROUPS = 4
D_STATE = 128
CONV_W = 4
CONV_DIM = SSD_INNER + 2 * SSD_GROUPS * D_STATE
SSD_BLOCK = 64
N_HEADS = 16
N_KV_HEADS = 4
HEAD_DIM = 64
IDX_HEADS = 8
IDX_DIM = 64
TOPK_MAX = 256
QUERY_BLOCK = 128
REL_BUCKETS = 32
REL_MAX_DIST = 128
D_FF = 4 * D_MODEL
ALPHA = (2.0 * DEPTH) ** 0.25
BETA = (8.0 * DEPTH) ** -0.25
LN_EPS = 1e-5
RMS_EPS = 1e-5
IN_SIZES = (SSD_INNER, CONV_DIM, SSD_HEADS,
            N_HEADS * HEAD_DIM, N_KV_HEADS * HEAD_DIM, N_KV_HEADS * HEAD_DIM,
            IDX_HEADS * IDX_DIM, IDX_DIM, IDX_HEADS,
            D_MODEL, D_MODEL)
IN_WIDTH = sum(IN_SIZES)
IN_SPLITS = tuple(int(v) for v in np.cumsum(IN_SIZES)[:-1])

kernel_name = 'hybrid_ssd_dsa_stream_step'


def layer_norm(x, g, b):
    xf = x.astype(jnp.float32)
    mu = jnp.mean(xf, axis=-1, keepdims=True)
    xc = xf - mu
    var = jnp.mean(xc * xc, axis=-1, keepdims=True)
    return (xc * lax.rsqrt(var + LN_EPS) * g.astype(jnp.float32) + b.astype(jnp.float32)).astype(x.dtype)


def gated_rmsnorm(y, z, w):
    h = (y * jax.nn.silu(z)).astype(jnp.float32)
    hg = h.reshape(h.shape[:-1] + (SSD_GROUPS, SSD_INNER // SSD_GROUPS))
    hg = hg * lax.rsqrt(jnp.mean(hg * hg, axis=-1, keepdims=True) + RMS_EPS)
    return (hg.reshape(h.shape) * w.astype(jnp.float32)).astype(y.dtype)


def t5_bucket(rel):
    half = REL_BUCKETS // 2
    max_exact = half // 2
    n = jnp.abs(rel)
    large = max_exact + (jnp.log(jnp.maximum(n, max_exact).astype(jnp.float32) / max_exact)
                         / math.log(REL_MAX_DIST / max_exact) * (half - max_exact)).astype(jnp.int32)
    large = jnp.minimum(large, half - 1)
    return jnp.where(rel > 0, half, 0) + jnp.where(n < max_exact, n, large)


def ssd_scan(x, dt, A, Bm, Cm, h0):
    b, l, h, p = x.shape
    g, n = Bm.shape[2], Bm.shape[3]
    r = h // g
    q = SSD_BLOCK
    pad = (-l) % q
    def padt(t):
        return jnp.pad(t, [(0, 0), (0, pad)] + [(0, 0)] * (t.ndim - 2))
    x, dt, Bm, Cm = padt(x), padt(dt), padt(Bm), padt(Cm)
    nc = (l + pad) // q
    xc = x.reshape(b, nc, q, g, r, p)
    dtc = dt.reshape(b, nc, q, g, r).astype(jnp.float32)
    Bc = Bm.reshape(b, nc, q, g, n)
    Cc = Cm.reshape(b, nc, q, g, n)
    a = dtc * A.reshape(g, r).astype(jnp.float32)
    acum = jnp.cumsum(a, axis=2)
    at = jnp.moveaxis(acum, 2, -1)
    seg = at[..., :, None] - at[..., None, :]
    causal = jnp.tril(jnp.ones((q, q), dtype=bool))
    decay = jnp.exp(jnp.where(causal, seg, -jnp.inf))
    cb = jnp.einsum('bcqgn,bcsgn->bcgqs', Cc, Bc).astype(jnp.float32)
    m = cb[:, :, :, None] * decay * jnp.moveaxis(dtc, 2, -1)[..., None, :]
    y_diag = jnp.einsum('bcgrqs,bcsgrp->bcqgrp', m, xc)
    w_end = jnp.exp(acum[:, :, -1:] - acum) * dtc
    s_c = jnp.einsum('bcsgn,bcsgrp->bcgrpn', Bc, xc * w_end[..., None])
    chunk_decay = jnp.exp(acum[:, :, -1])

    def step(hc, inp):
        dec, sc = inp
        return hc * dec[..., None, None] + sc, hc

    h_init = h0.reshape(b, g, r, p, n).astype(jnp.float32)
    h_last, h_prev = lax.scan(step, h_init, (jnp.moveaxis(chunk_decay, 1, 0), jnp.moveaxis(s_c, 1, 0)))
    h_prev = jnp.moveaxis(h_prev, 0, 1)
    y_off = jnp.einsum('bcqgn,bcgrpn->bcqgrp', Cc, h_prev) * jnp.exp(acum)[..., None]
    y = (y_diag + y_off).reshape(b, nc * q, h, p)[:, :l]
    return y, h_last.reshape(b, h, p, n)


def dsa_attention(q, qi, wi, k_all, v_all, ki_all, rel_bias, q_start):
    b, t = q.shape[0], q.shape[1]
    s = k_all.shape[1]
    n_sel = min(TOPK_MAX, s // 4)
    qb = QUERY_BLOCK if t % QUERY_BLOCK == 0 else t
    nb = t // qb
    grp = N_HEADS // N_KV_HEADS
    key_pos = jnp.arange(s, dtype=jnp.int32)
    bidx = jnp.arange(b)[:, None, None]

    def blocks(arr):
        return jnp.moveaxis(arr.reshape((b, nb, qb) + arr.shape[2:]), 1, 0)

    q_pos = (q_start + jnp.arange(t, dtype=jnp.int32)).reshape(nb, qb)

    def one_block(args):
        q_b, qi_b, wi_b, pos_b = args
        visible_end = (pos_b // CHUNK + 1) * CHUNK
        dots = jnp.einsum('bthd,bsd->bths', qi_b, ki_all).astype(jnp.float32) * IDX_DIM ** -0.5
        score = jnp.einsum('bth,bths->bts', wi_b.astype(jnp.float32) * IDX_HEADS ** -0.5, jax.nn.relu(dots))
        visible = key_pos[None, :] < visible_end[:, None]
        score = jnp.where(visible[None], score, -jnp.inf)
        _, sel = lax.top_k(score, n_sel)
        valid = sel < visible_end[None, :, None]
        k_sel = k_all[bidx, sel]
        v_sel = v_all[bidx, sel]
        qg = q_b.reshape(b, qb, N_KV_HEADS, grp, HEAD_DIM)
        logits = jnp.einsum('btkgd,btskd->btkgs', qg, k_sel).astype(jnp.float32) * HEAD_DIM ** -0.5
        bias = rel_bias[t5_bucket(sel - pos_b[None, :, None])]
        bias = jnp.moveaxis(bias, 2, 3).reshape(b, qb, N_KV_HEADS, grp, n_sel)
        logits = jnp.where(valid[:, :, None, None, :], logits + bias.astype(jnp.float32), -jnp.inf)
        probs = jax.nn.softmax(logits, axis=-1).astype(v_sel.dtype)
        out = jnp.einsum('btkgs,btskd->btkgd', probs, v_sel)
        return out.reshape(b, qb, N_HEADS * HEAD_DIM)

    out = lax.map(one_block, (blocks(q), blocks(qi), blocks(wi), q_pos))
    return jnp.moveaxis(out, 0, 1).reshape(b, t, N_HEADS * HEAD_DIM)


def hybrid_layer(x, cache_k, cache_v, cache_kidx, state_ssm, state_conv, rel_bias,
                 w_in, conv_w, conv_b, dt_bias, a_log, d_skip, ssd_norm_w, w_ssd_o, w_attn_o, w_out,
                 ln1_g, ln1_b, w_up, w_down, ln2_g, ln2_b):
    b, t, _ = x.shape
    past = cache_k.shape[1]
    proj = x @ w_in
    z, xbc, dt, q, k, v, qi, ki, wi, g_ssd, g_attn = jnp.split(proj, IN_SPLITS, axis=-1)

    xbc_pad = jnp.concatenate([state_conv.astype(xbc.dtype), xbc], axis=1)
    new_conv = xbc_pad[:, -(CONV_W - 1):]
    conv = conv_b
    for i in range(CONV_W):
        conv = conv + xbc_pad[:, i:i + t] * conv_w[i]
    xbc_act = jax.nn.silu(conv)
    xs, Bm, Cm = jnp.split(xbc_act, (SSD_INNER, SSD_INNER + SSD_GROUPS * D_STATE), axis=-1)
    xs = xs.reshape(b, t, SSD_HEADS, SSD_HEAD_DIM)
    Bm = Bm.reshape(b, t, SSD_GROUPS, D_STATE)
    Cm = Cm.reshape(b, t, SSD_GROUPS, D_STATE)
    dt_pos = jax.nn.softplus(dt.astype(jnp.float32) + dt_bias.astype(jnp.float32))
    A = -jnp.exp(a_log.astype(jnp.float32))
    y_ssd, new_ssm = ssd_scan(xs, dt_pos, A, Bm, Cm, state_ssm)
    y_ssd = (y_ssd + d_skip[:, None] * xs).astype(x.dtype).reshape(b, t, SSD_INNER)
    y_ssd = gated_rmsnorm(y_ssd, z, ssd_norm_w)
    y1 = y_ssd @ w_ssd_o

    k_new = k.reshape(b, t, N_KV_HEADS, HEAD_DIM)
    v_new = v.reshape(b, t, N_KV_HEADS, HEAD_DIM)
    k_all = jnp.concatenate([cache_k.astype(k_new.dtype), k_new], axis=1)
    v_all = jnp.concatenate([cache_v.astype(v_new.dtype), v_new], axis=1)
    ki_all = jnp.concatenate([cache_kidx.astype(ki.dtype), ki], axis=1)
    o = dsa_attention(q.reshape(b, t, N_HEADS, HEAD_DIM), qi.reshape(b, t, IDX_HEADS, IDX_DIM), wi,
                      k_all, v_all, ki_all, rel_bias, past)
    y2 = o @ w_attn_o

    mixed = (jax.nn.sigmoid(g_ssd) * y1 + jax.nn.sigmoid(g_attn) * y2) @ w_out
    h = layer_norm(ALPHA * x + mixed, ln1_g, ln1_b)
    f = jnp.square(jax.nn.relu(h @ w_up)) @ w_down
    out = layer_norm(ALPHA * h + f, ln2_g, ln2_b)
    return out, (k_new, v_new, ki, new_ssm.astype(x.dtype), new_conv)


def setup_inputs(seed: int = 0) -> dict:
    key = jax.random.key(seed)
    ks = jax.random.split(key, 32)
    f32 = jnp.float32

    def nrm(k, shape, scale):
        return jax.random.normal(k, shape, f32) * scale

    L = DEPTH
    dt0 = jnp.exp(jax.random.uniform(ks[11], (L, SSD_HEADS), f32, math.log(1e-3), math.log(1e-1)))
    return {
        'x_prompt': nrm(ks[0], (BATCH, SEQ, D_MODEL), 1.0),
        'x_sample': nrm(ks[1], (DEC_BATCH, DEC_SEQ, D_MODEL), 1.0),
        'cache_k': nrm(ks[2], (L, DEC_BATCH, PAST_LEN, N_KV_HEADS, HEAD_DIM), 1.0),
        'cache_v': nrm(ks[3], (L, DEC_BATCH, PAST_LEN, N_KV_HEADS, HEAD_DIM), 1.0),
        'cache_kidx': nrm(ks[4], (L, DEC_BATCH, PAST_LEN, IDX_DIM), 1.0),
        'state_ssm': nrm(ks[5], (L, DEC_BATCH, SSD_HEADS, SSD_HEAD_DIM, D_STATE), 0.1),
        'state_conv': nrm(ks[6], (L, DEC_BATCH, CONV_W - 1, CONV_DIM), 1.0),
        'rel_bias': nrm(ks[7], (REL_BUCKETS, N_HEADS), 0.5),
        'w_in': nrm(ks[8], (L, D_MODEL, IN_WIDTH), D_MODEL ** -0.5),
        'conv_w': nrm(ks[9], (L, CONV_W, CONV_DIM), CONV_W ** -0.5),
        'conv_b': nrm(ks[10], (L, CONV_DIM), 0.01),
        'dt_bias': dt0 + jnp.log(-jnp.expm1(-dt0)),
        'a_log': jnp.log(jax.random.uniform(ks[12], (L, SSD_HEADS), f32, 1.0, 16.0)),
        'd_skip': 1.0 + nrm(ks[13], (L, SSD_HEADS), 0.01),
        'ssd_norm_w': 1.0 + nrm(ks[14], (L, SSD_INNER), 0.01),
        'w_ssd_o': nrm(ks[15], (L, SSD_INNER, D_MODEL), SSD_INNER ** -0.5 * BETA),
        'w_attn_o': nrm(ks[16], (L, N_HEADS * HEAD_DIM, D_MODEL), (N_HEADS * HEAD_DIM) ** -0.5 * BETA),
        'w_out': nrm(ks[17], (L, D_MODEL, D_MODEL), D_MODEL ** -0.5 * BETA),
        'ln1_g': 1.0 + nrm(ks[18], (L, D_MODEL), 0.01),
        'ln1_b': nrm(ks[19], (L, D_MODEL), 0.01),
        'w_up': nrm(ks[20], (L, D_MODEL, D_FF), D_MODEL ** -0.5),
        'w_down': nrm(ks[21], (L, D_FF, D_MODEL), D_FF ** -0.5 * BETA),
        'ln2_g': 1.0 + nrm(ks[22], (L, D_MODEL), 0.01),
        'ln2_b': nrm(ks[23], (L, D_MODEL), 0.01),
    }


def reference(x_prompt, x_sample, cache_k, cache_v, cache_kidx, state_ssm, state_conv, rel_bias,
              w_in, conv_w, conv_b, dt_bias, a_log, d_skip, ssd_norm_w, w_ssd_o, w_attn_o, w_out,
              ln1_g, ln1_b, w_up, w_down, ln2_g, ln2_b):
    bp = x_prompt.shape[0]
    dtype = x_prompt.dtype
    empty_k = jnp.zeros((bp, 0, N_KV_HEADS, HEAD_DIM), dtype)
    empty_ki = jnp.zeros((bp, 0, IDX_DIM), dtype)
    zero_ssm = jnp.zeros((bp, SSD_HEADS, SSD_HEAD_DIM, D_STATE), dtype)
    zero_conv = jnp.zeros((bp, CONV_W - 1, CONV_DIM), dtype)
    yp, ys = x_prompt, x_sample
    st_p, st_s = [], []
    for l in range(DEPTH):
        yp, sp = hybrid_layer(yp, empty_k, empty_k, empty_ki, zero_ssm, zero_conv, rel_bias,
                              w_in[l], conv_w[l], conv_b[l], dt_bias[l], a_log[l], d_skip[l], ssd_norm_w[l],
                              w_ssd_o[l], w_attn_o[l], w_out[l], ln1_g[l], ln1_b[l], w_up[l], w_down[l],
                              ln2_g[l], ln2_b[l])
        ys, ss = hybrid_layer(ys, cache_k[l], cache_v[l], cache_kidx[l], state_ssm[l], state_conv[l], rel_bias,
                              w_in[l], conv_w[l], conv_b[l], dt_bias[l], a_log[l], d_skip[l], ssd_norm_w[l],
                              w_ssd_o[l], w_attn_o[l], w_out[l], ln1_g[l], ln1_b[l], w_up[l], w_down[l],
                              ln2_g[l], ln2_b[l])
        st_p.append(sp)
        st_s.append(ss)
    k_p = jnp.stack([s[0] for s in st_p])
    v_p = jnp.stack([s[1] for s in st_p])
    ki_p = jnp.stack([s[2] for s in st_p])
    ssm_p = jnp.stack([s[3] for s in st_p])
    conv_p = jnp.stack([s[4] for s in st_p])
    k_s = jnp.stack([s[0] for s in st_s])
    v_s = jnp.stack([s[1] for s in st_s])
    ki_s = jnp.stack([s[2] for s in st_s])
    ssm_s = jnp.stack([s[3] for s in st_s])
    conv_s = jnp.stack([s[4] for s in st_s])
    return (yp, ys, k_p, v_p, ki_p, ssm_p, conv_p, k_s, v_s, ki_s, ssm_s, conv_s)
```

```python
import contextlib
import math
import numpy as np
import concourse.bass as bass
import concourse.mybir as mybir
from concourse.bass_utils import run_bass_kernel_spmd

F32 = mybir.dt.float32
BF16 = mybir.dt.bfloat16
AF = mybir.ActivationFunctionType
ALU = mybir.AluOpType
AX = mybir.AxisListType

SAME_ENG_SYNC = True
import os
NOSYNC_ENG = set(os.environ.get('K_NOSYNC', 'pe').split(','))
STORES_ON_POOL = True
NDSEM = 32


class Buf:
    def __init__(self, t, name=""):
        self.t = t
        self.name = name
        self.w = None
        self.r = {}

    def __getitem__(self, k):
        return self.t[k]


class View:
    def __init__(self, parent, ap):
        self.p = parent
        self.t = ap

    def __getitem__(self, k):
        return self.t[k]

    @property
    def w(self):
        return self.p.w

    @w.setter
    def w(self, v):
        self.p.w = v

    @property
    def r(self):
        return self.p.r

    @r.setter
    def r(self, v):
        self.p.r = v


class FW:
    def __init__(self, nc, stack):
        self.nc = nc
        self.stack = stack
        self.engs = ["pe", "act", "dve", "pool", "sp"]
        self.sems = {}
        self.cnt = {}
        self.prog = {k: [] for k in self.engs}
        self.waited = {k: {} for k in self.engs}
        for k in self.engs:
            self.sems[k] = stack.enter_context(nc.semaphore("s_" + k))
            self.cnt[k] = 0
        self.dcnt = []
        for i in range(NDSEM):
            self.sems["d%d" % i] = stack.enter_context(nc.semaphore("d%d" % i))
            self.dcnt.append(0)
        self.dnext = 0
        self.nbuf = 0

    def sb(self, shape, dt, name=None):
        self.nbuf += 1
        name = name or ("sb%d" % self.nbuf)
        return Buf(self.stack.enter_context(self.nc.sbuf_tensor(name, list(shape), dt)), name)

    def ps(self, shape, dt, name=None):
        self.nbuf += 1
        name = name or ("ps%d" % self.nbuf)
        return Buf(self.stack.enter_context(self.nc.psum_tensor(name, list(shape), dt)), name)

    def _deps(self, e, reads, writes):
        need = {}

        def add(dep):
            if dep is None:
                return
            k, v = dep
            if k == e and (not SAME_ENG_SYNC or e in NOSYNC_ENG):
                return
            if need.get(k, 0) < v:
                need[k] = v

        for b in reads:
            add(b.w)
        for b in writes:
            add(b.w)
            for k, v in b.r.items():
                add((k, v))
        out = []
        wd = self.waited[e]
        for k, v in need.items():
            if wd.get(k, 0) >= v:
                continue
            wd[k] = v
            out.append((k, v))
        return out

    def op(self, e, fn, reads=(), writes=()):
        waits = self._deps(e, reads, writes)
        self.cnt[e] += 1
        v = self.cnt[e]
        self.prog[e].append((waits, fn, (e, 1)))
        for b in reads:
            if b.r.get(e, 0) < v:
                b.r[e] = v
        for b in writes:
            b.w = (e, v)
            b.r = {}

    def dma(self, out_ap, in_ap, reads=(), writes=(), q=None):
        if q is None:
            q = "pool" if (STORES_ON_POOL and "DRam" in type(out_ap.tensor).__name__) else "sp"
        i = self.dnext
        self.dnext = (self.dnext + 1) % NDSEM
        key = "d%d" % i
        waits = self._deps(q, reads, writes)
        prev = self.dcnt[i]
        if prev > 0 and self.waited[q].get(key, 0) < prev:
            self.waited[q][key] = prev
            waits.append((key, prev))
        self.dcnt[i] += 16
        v = self.dcnt[i]
        self.prog[q].append((waits, lambda eng: eng.dma_start(out=out_ap, in_=in_ap), (key, 16)))
        for b in reads:
            if b.r.get(key, 0) < v:
                b.r[key] = v
        for b in writes:
            b.w = (key, v)
            b.r = {}

    def finish(self):
        waits = []
        for i in range(NDSEM):
            key = "d%d" % i
            if self.dcnt[i] > 0 and self.waited["sp"].get(key, 0) < self.dcnt[i]:
                waits.append((key, self.dcnt[i]))
        self.prog["sp"].append((waits, None, None))
        nc = self.nc
        with nc.Block() as block:
            for ename, deco in (("sp", block.sync), ("pe", block.tensor), ("act", block.scalar),
                                ("dve", block.vector), ("pool", block.gpsimd)):
                prog = self.prog[ename]

                def body(eng, prog=prog):
                    for waits, fn, inc in prog:
                        for k, v in waits:
                            eng.wait_ge(self.sems[k], v)
                        if fn is None:
                            continue
                        ins = fn(eng)
                        ins.then_inc(self.sems[inc[0]], inc[1])

                deco(body)


D = 1024
KC = 8
NW = 9320
C_Z, C_XBC, C_DT, C_Q, C_K, C_V, C_QI, C_KI, C_WI, C_G1, C_G2 = 0, 2048, 5120, 5152, 6176, 6432, 6688, 7200, 7264, 7272, 8296
NREL = 768
TOPK = 256
ALPHA = 2.0 ** 0.25
NBIS = 25
NEG = -30000.0


def bcast(ap, shape):
    return ap.to_broadcast(list(shape))


def build(SEQ, PAST):
    NB = SEQ // 128
    NSLOT = NB // 4
    SPS = PAST + 128
    NKC_S = PAST // 128 + 1
    nc = bass.Bass("TRN2", target_bir_lowering=False)

    def din(name, shape, dt=F32):
        return Buf(nc.dram_tensor(name, list(shape), dt, kind="ExternalInput").ap(), name)

    def dout(name, shape, dt=F32):
        return Buf(nc.dram_tensor(name, list(shape), dt, kind="ExternalOutput").ap(), name)

    def dint(name, shape, dt):
        return Buf(nc.dram_tensor(name, list(shape), dt, kind="Internal").ap(), name)

    xT_p = din("xT_p", [128, KC, 3 + SEQ])
    xrow_p = din("xrow_p", [NSLOT, 128, D])
    xTo_p = din("xTo_p", [NSLOT, 128, KC, 131])
    xT_s = din("xT_s", [2, 128, KC, 134])
    xrow_s = din("xrow_s", [2, 128, D])
    convst_s = din("convst_s", [2, 128, 24, 2, 3])
    rowvalid = din("rowvalid", [128, 1])
    ssmT_s = din("ssmT_s", [4, 128, 2048])
    ckT_s = din("ckT_s", [4, 64, 4, PAST])
    cv_s = din("cv_s", [4, PAST, 256])
    ckiT_s = din("ckiT_s", [4, 64, PAST])
    w_in = din("w_in", [D, NW])
    w_ssd = din("w_ssd", [2048, D])
    w_att = din("w_att", [D, D])
    w_o = din("w_o", [D, D])
    w_up = din("w_up", [D, 4096])
    w_dn = din("w_dn", [4096, D])
    convw = din("convw", [128, 24, 4])
    convb = din("convb", [128, 24])
    dtb = din("dtb", [128, 32])
    alog = din("alog", [128, 32])
    dsk = din("dsk", [128, 32])
    normw = din("normw", [128, 16])
    lnp = din("lnp", [4, 128, D])
    relb = din("relb", [32, 16])
    onehot = din("onehot", [2, 32, NREL])
    vispen = din("vispen", [2, 128, 4, 128])
    oh4 = din("oh4", [128, 4])
    cmats = din("cmats", [2, 3, 128, 128])
    idrep_in = din("idrep", [3, 128, 4, 128])
    y_p = dout("y_p", [NSLOT, 128, D])
    y_s = dout("y_s", [2, 128, D])
    k_p = dout("k_p", [SEQ, 256])
    v_p = dout("v_p", [SEQ, 256])
    ki_p = dout("ki_p", [SEQ, 64])
    ssm_p = dout("ssm_p", [128, 2048])
    conv_p = dout("conv_p", [128, 24, 3])
    k_s = dout("k_s", [256, 256])
    v_s = dout("v_s", [256, 256])
    ki_s = dout("ki_s", [256, 64])
    ssm_s = dout("ssm_s", [4, 128, 2048])
    conv_s = dout("conv_s", [2, 128, 24, 2, 3])
    NT = 96
    wtiles = dint("wtiles", [NT, 128, 4096], BF16)
    BTs = dint("BTs", [7, 128, 2048], BF16)
    KTp = dint("KTp", [64, 4, SEQ], BF16)
    Vp = dint("Vp", [SEQ, 260], BF16)
    kiTp = dint("kiTp", [64, SEQ], BF16)
    KTs = [dint("KTs%d" % i, [64, 4, SPS], BF16) for i in range(4)]
    Vs = [dint("Vs%d" % i, [SPS, 260], BF16) for i in range(4)]
    kiTs = [dint("kiTs%d" % i, [64, SPS], BF16) for i in range(4)]
    TR = dint("TR", [2, 16, NREL], F32)

    with contextlib.ExitStack() as st:
        fw = FW(nc, st)
        op, dma = fw.op, fw.dma
        SMAX = max(SEQ, SPS)
        ident = fw.sb([128, 128], BF16, "ident")
        antiI = fw.sb([128, 128], BF16, "antiI")
        zeros = fw.sb([128, 512], BF16, "zeros")
        onesf = fw.sb([128, 128], F32, "onesf")
        cm = [[fw.sb([128, 128], F32, "cm%d%d" % (a, b)) for b in range(3)] for a in range(2)]
        penrep = [fw.sb([128, 4, 128], BF16, "penrep%d" % a) for a in range(2)]
        idrep = [fw.sb([128, 4, 128], BF16, "idrep0"), fw.sb([128, 4, 64], BF16, "idrep1"), fw.sb([128, 4, 64], BF16, "idrep2")]
        identf = fw.sb([128, 128], F32, "identf")
        vpen = [fw.sb([128, 4, 128], F32, "vpen%d" % a) for a in range(2)]
        oh4s = fw.sb([128, 4], F32, "oh4s")
        rvs = fw.sb([128, 1], F32, "rvs")
        cw = fw.sb([128, 24, 4], F32, "cw")
        cb = fw.sb([128, 24], F32, "cb")
        dtbs = fw.sb([128, 32], F32, "dtbs")
        Aneg = fw.sb([128, 32], F32, "Aneg")
        dsks = fw.sb([128, 32], F32, "dsks")
        nws = fw.sb([128, 16], F32, "nws")
        wt = [fw.sb([128, 8, 512], BF16, "wt%d" % i) for i in range(2)]
        wti = [0]
        xTb = fw.sb([128, KC, 140], BF16, "xTb")
        xTo = fw.sb([128, KC, 128], BF16, "xTo")
        pre = fw.sb([128, 24, 140], F32, "pre")
        xTf = pre
        acc2 = [fw.sb([128, 128], F32, "acc%d" % i) for i in range(2)]
        xc = fw.sb([128, 24, 128], BF16, "xc")
        xtok = fw.sb([128, 2048], BF16, "xtok")
        btok = fw.sb([128, 512], BF16, "btok")
        wx = fw.sb([128, 2048], BF16, "wx")
        small = {n: fw.sb([128, 32], F32, "sm_" + n) for n in
                 ["xb", "ab", "e", "l", "dt", "a", "acum", "nacum", "atot", "w", "eA", "dec", "tmp"]}
        H = fw.sb([128, 2048], F32, "H")
        Hown = fw.sb([128, 2048], BF16, "Hown")
        kvf = fw.sb([128, 576], F32, "kvf")
        v1 = fw.sb([128, 4, 65], BF16, "v1")
        ktb = fw.sb([64, 5, 128], BF16, "ktb")
        QT = fw.sb([64, 2048], BF16, "QT")
        qiT = fw.sb([64, 8, 128], BF16, "qiT")
        wis = fw.sb([128, 8], F32, "wis")
        zg = fw.sb([128, 4096], BF16, "zg")
        yo = fw.sb([128, 2048], F32, "yo")
        ysb = fw.sb([128, 2048], F32, "ysb")
        D8 = fw.sb([128, 8, 128], F32, "D8")
        ynb = wx
        st4 = fw.sb([128, 8], F32, "st4")
        sc = fw.sb([128, max(SMAX, 4096)], F32, "sc")
        junk = fw.sb([128, 4096], BF16, "junk")
        rl = [Buf(D8[:, 0:4, :].rearrange("p h q -> p (h q)"), "rlA"), Buf(D8[:, 4:8, :].rearrange("p h q -> p (h q)"), "rlB")]
        acc4 = [Buf(D8[:, i, :], "acc4_%d" % i) for i in range(4)]
        scT = [Buf(sc.t, "scA"), Buf(sc.t, "scB")]
        bis = {n: fw.sb([128, 1], F32, "bis_" + n) for n in ["lo", "hi", "mid", "cnt", "cond", "ncond", "d", "e", "B"]}
        cnt4 = fw.sb([128, 8], F32, "cnt4")
        mbt = [fw.sb([128, 512], BF16, "mbt0"), btok]
        kit = [fw.sb([64, 512], BF16, "kit%d" % i) for i in range(2)]
        ktt = [fw.sb([64, 4, 512], BF16, "ktt0"), View(wx, wx[0:64, :].rearrange("p (k s) -> p k s", k=4))]
        vtt = [fw.sb([128, 4, 260], BF16, "vtt0"), View(xTb, xTb[:, :, :].rearrange("p k c -> p (k c)")[:, 0:1040].rearrange("p (c d) -> p c d", c=4))]
        bt = fw.sb([128, 16, 128], BF16, "bt")
        PT = [fw.sb([128, 8, 128], BF16, "PT0"), xTo]
        MT = PT[0]
        Dec = View(bt, bt[:, 0:8, :])
        ynT = bt
        rinv = fw.sb([128, 16], F32, "rinv")
        tT = fw.sb([128, 8, 128], BF16, "tT")
        mb16 = fw.sb([128, D], BF16, "mb16")
        osb = mb16
        rb = fw.sb([32, 16], F32, "rb")
        cfar = fw.sb([16, 1], F32, "cfar")
        stg = [View(sc, sc[:, 0:2048]), View(sc, sc[:, 2048:4096])]
        stgb = [View(junk, junk[:, 0:2048]), View(junk, junk[:, 2048:4096])]
        zs = View(zg, zg[:, 0:2048])
        gate = View(zg, zg[:, 2048:4096])
        fT = View(zg, zg[:, :].rearrange("p (j q) -> p j q", q=128))
        btf = View(yo, yo[:, :].rearrange("p (h q) -> p h q", q=128))
        m1 = View(ysb, ysb[:, D:2 * D])
        xrow = View(yo, yo[:, 0:D])
        trs = View(sc, sc[0:16, 0:NREL])
        ohs = View(sc, sc[0:32, 1024:1024 + NREL])
        hs = View(yo, yo[:, D:2 * D])
        presub = [Buf(pre.t, "pre%d" % j) for j in range(24)]
        xcsub = [Buf(xc.t, "xc%d" % j) for j in range(24)]
        PB = [fw.ps([128, 512], F32, "pb%d" % i) for i in range(7)]
        PTR = fw.ps([128, 1024], BF16, "ptr")
        PTRS = [Buf(PTR.t, "ptr%d" % i) for i in range(8)]
        btH = [Buf(bt.t, "btH0"), Buf(bt.t, "btH1")]
        mTs = [fw.sb([128, 128], BF16, "mTs%d" % i) for i in range(2)]

        def nextw():
            wti[0] = (wti[0] + 1) % 2
            return wt[wti[0]]

        def mm(out, lhsT, rhs, start, stop, reads, writes):
            op("pe", lambda e: e.matmul(out, lhsT=lhsT, rhs=rhs, start=start, stop=stop, skip_group_check=True),
               reads=reads, writes=writes)

        def tr(out, in_, reads, writes, idn=None):
            n = in_.shape[0]
            op("pe", lambda e: e.transpose(out, in_, ident[:n, :n]), reads=list(reads) + [ident], writes=writes)

        def act(out, in_, func, reads, writes, **kw):
            op("act", lambda e: e.activation(out=out, in_=in_, func=func, **kw), reads=reads, writes=writes)

        def tt(e_, out, in0, in1, o, reads, writes):
            op(e_, lambda e: e.tensor_tensor(out=out, in0=in0, in1=in1, op=o), reads=reads, writes=writes)

        def ts(e_, out, in0, s1, s2, op0, op1, reads, writes, **kw):
            if op1 is None:
                op(e_, lambda e: e.tensor_scalar(out=out, in0=in0, scalar1=s1, scalar2=None, op0=op0, **kw), reads=reads, writes=writes)
            else:
                op(e_, lambda e: e.tensor_scalar(out=out, in0=in0, scalar1=s1, scalar2=s2, op0=op0, op1=op1, **kw), reads=reads, writes=writes)

        def stt(out, in0, scalar, in1, op0, op1, reads, writes):
            op("dve", lambda e: e.scalar_tensor_tensor(out=out, in0=in0, scalar=scalar, in1=in1, op0=op0, op1=op1),
               reads=reads, writes=writes)

        def cp(e_, out, in_, reads, writes):
            if e_ == "act":
                act(out, in_, AF.Copy, reads, writes)
            else:
                op(e_, lambda e: e.tensor_copy(out=out, in_=in_), reads=reads, writes=writes)

        op("pool", lambda e: e.memset(ident[:], 0.0), writes=[ident])
        op("pool", lambda e: e.affine_select(out=ident[:], in_=ident[:], pattern=[[-1, 128]], compare_op=ALU.not_equal,
                                             fill=1.0, base=0, channel_multiplier=1), reads=[ident], writes=[ident])
        op("pool", lambda e: e.memset(antiI[:], 0.0), writes=[antiI])
        op("pool", lambda e: e.affine_select(out=antiI[:], in_=antiI[:], pattern=[[1, 128]], compare_op=ALU.not_equal,
                                             fill=1.0, base=-127, channel_multiplier=1), reads=[antiI], writes=[antiI])
        op("pool", lambda e: e.memset(zeros[:], 0.0), writes=[zeros])
        op("pool", lambda e: e.memset(onesf[:], 1.0), writes=[onesf])
        cp("dve", identf[:], ident[:], [ident], [identf])
        for a in range(2):
            for b in range(3):
                dma(cm[a][b][:], cmats[a, b], writes=[cm[a][b]])
            cp("dve", penrep[a][:], bcast(cm[a][2][:].unsqueeze(1), [128, 4, 128]), [cm[a][2]], [penrep[a]])
        for a in range(3):
            dma(stg[0][:, 0:512], idrep_in[a].rearrange("p g q -> p (g q)"), writes=[stg[0]])
            cp("dve", idrep[a][:], stg[0][:, 0:512].rearrange("p (g q) -> p g q", g=4)[:, :, 0:(128 if a == 0 else 64)], [stg[0]], [idrep[a]])
        for a in range(2):
            dma(vpen[a][:], vispen[a], writes=[vpen[a]])
        dma(oh4s[:], oh4[:], writes=[oh4s])
        dma(rvs[:], rowvalid[:], writes=[rvs])
        dma(cw[:], convw[:], writes=[cw])
        dma(cb[:], convb[:], writes=[cb])
        dma(dtbs[:], dtb[:], writes=[dtbs])
        dma(Aneg[:], alog[:], writes=[Aneg])
        act(Aneg[:], Aneg[:], AF.Exp, [Aneg], [Aneg])
        ts("dve", Aneg[:], Aneg[:], -1.0, None, ALU.mult, None, [Aneg], [Aneg])
        dma(dsks[:], dsk[:], writes=[dsks])
        dma(nws[:], normw[:], writes=[nws])
        dma(rb[:], relb[:], writes=[rb])
        dma(cfar[:], relb[15:16, :].rearrange("o h -> h o"), writes=[cfar])
        for t in range(2):
            dma(ohs[:], onehot[t], writes=[ohs])
            for hlf in range(2):
                mm(PB[0][:16, 0:384], rb[:], ohs[:, hlf * 384:(hlf + 1) * 384], True, True, [rb, ohs], [PB[0]])
                ts("dve", trs[:, hlf * 384:(hlf + 1) * 384], PB[0][:16, 0:384], cfar[:, 0:1], None, ALU.subtract, None,
                   [PB[0], cfar], [trs])
            dma(TR[t], trs[:], reads=[trs], writes=[TR])

        ci = [0]

        wtab = {}

        def wload(w, src, k0, c0, ncols):
            key = (src.name, k0, c0, ncols)
            if key not in wtab:
                idx = len(wtab)
                assert idx < NT
                wtab[key] = idx
                dma(sc[:, 0:8 * ncols].rearrange("p (k c) -> p k c", k=8),
                    src[k0 * 128:(k0 + 8) * 128, c0:c0 + ncols].rearrange("(k p) c -> p k c", p=128), writes=[sc])
                cp("act", junk[:, 0:8 * ncols], sc[:, 0:8 * ncols], [sc], [junk])
                dma(wtiles[idx, :, 0:8 * ncols], junk[:, 0:8 * ncols], reads=[junk], writes=[wtiles])
            idx = wtab[key]
            if ncols == 512:
                dma(w[:, :, :].rearrange("p k c -> p (k c)"), wtiles[idx, :, :], reads=[wtiles], writes=[w])
            else:
                dma(w[:, :, :ncols], wtiles[idx, :, 0:8 * ncols].rearrange("p (k c) -> p k c", k=8), reads=[wtiles], writes=[w])

        for bi in range(7):
            tb_, cc_, n_ = (0, bi, 128) if bi < 5 else (1, bi - 5, 64)
            src = bass.AP(tensor=TR.t.tensor, offset=tb_ * 16 * NREL + 128 * (4 - cc_), ap=[[1, 128], [NREL, 16], [1, n_]])
            dma(sc[:, 0:16 * n_].rearrange("p (h q) -> p h q", q=n_), src, reads=[TR], writes=[sc])
            cp("act", junk[:, 0:16 * n_], sc[:, 0:16 * n_], [sc], [junk])
            dma(BTs[bi, :, 0:16 * n_], junk[:, 0:16 * n_], reads=[junk], writes=[BTs])

        for s_ in range(4):
            for p0 in range(0, PAST, 512):
                i = ci[0] % 2
                ci[0] += 1
                dma(stg[i][:64, :].rearrange("p (k s) -> p k s", k=4), ckT_s[s_, :, :, p0:p0 + 512], writes=[stg[i]])
                cp("act", stgb[i][:64, :], stg[i][:64, :], [stg[i]], [stgb[i]])
                dma(KTs[s_][:, :, p0:p0 + 512], stgb[i][:64, :].rearrange("p (k s) -> p k s", k=4), reads=[stgb[i]], writes=[KTs[s_]])
                i = ci[0] % 2
                ci[0] += 1
                dma(stg[i][:, 0:1024].rearrange("p (c d) -> p c d", c=4),
                    cv_s[s_, p0:p0 + 512, :].rearrange("(c p) d -> p c d", p=128), writes=[stg[i]])
                op("pool", lambda e, i=i: e.memset(stgb[i][:, 0:1040], 1.0), writes=[stgb[i]])
                cp("dve", stgb[i][:, 0:1040].rearrange("p (c k d) -> p c k d", c=4, k=4)[:, :, :, 0:64],
                   stg[i][:, 0:1024].rearrange("p (c k d) -> p c k d", c=4, k=4), [stg[i]], [stgb[i]])
                dma(Vs[s_][p0:p0 + 512, :].rearrange("(c p) d -> p c d", p=128),
                    stgb[i][:, 0:1040].rearrange("p (c d) -> p c d", c=4), reads=[stgb[i]], writes=[Vs[s_]])
                i = ci[0] % 2
                ci[0] += 1
                dma(stg[i][:64, 0:512], ckiT_s[s_, :, p0:p0 + 512], writes=[stg[i]])
                cp("act", stgb[i][:64, 0:512], stg[i][:64, 0:512], [stg[i]], [stgb[i]])
                dma(kiTs[s_][:, p0:p0 + 512], stgb[i][:64, 0:512], reads=[stgb[i]], writes=[kiTs[s_]])
            dma(KTs[s_][:, :, PAST:PAST + 128], zeros[:64, :].rearrange("p (k s) -> p k s", k=4), reads=[zeros], writes=[KTs[s_]])
            dma(Vs[s_][PAST:PAST + 128, :], zeros[:, 0:260], reads=[zeros], writes=[Vs[s_]])
            dma(kiTs[s_][:, PAST:PAST + 128], zeros[:64, 0:128], reads=[zeros], writes=[kiTs[s_]])

        pbi = [0]

        def bank():
            pbi[0] = (pbi[0] + 1) % 4
            return PB[pbi[0]]

        bg = [None]

        def pump(nsteps):
            for _ in range(nsteps):
                if bg[0] is None:
                    return
                try:
                    next(bg[0])
                except StopIteration:
                    bg[0] = None

        def flush():
            while bg[0] is not None:
                pump(1)

        def run(gen):
            for _ in gen:
                pass

        def job(kind, full, xT_src, segs, blk_i=None, outs=None, aux=None):
            nseg = 1 if kind == 0 else 2
            n = 128 // nseg
            ncol = nseg * (3 + n)
            CM = cm[kind]

            def own(t3):
                return t3.rearrange("p (s c) -> p s c", c=3 + n)[:, :, 3:]

            dma(xTf[:, 0:KC, :ncol], xT_src, writes=presub[0:KC])
            cp("act", xTb[:, :, :ncol], xTf[:, 0:KC, :ncol], presub[0:KC], [xTb])

            def feat(c0, M, nchunk, evac, rhs_cols=None):
                per = 512 // M
                for j0 in range(0, nchunk, per):
                    nj = min(per, nchunk - j0)
                    w = nextw()
                    wload(w, w_in, 0, c0 + j0 * M, nj * M)
                    for j in range(j0, j0 + nj):
                        pb = bank()
                        for kc in range(KC):
                            mm(pb[:M, :ncol], w[:, kc, (j - j0) * M:(j - j0 + 1) * M], xTb[:, kc, :ncol], kc == 0, kc == KC - 1,
                               [w, xTb], [pb])
                        evac(j, pb)
                    yield

            def tokmm(wd, c0, ncols, lhs, nkc, evac, reads):
                pb = bank()
                for k0 in range(0, nkc, 8):
                    w = nextw()
                    wload(w, wd, k0, c0, ncols)
                    for kc in range(8):
                        mm(pb[:, :ncols], lhs(k0 + kc), w[:, kc, :ncols], k0 + kc == 0, k0 + kc == nkc - 1, [w] + reads, [pb])
                evac(pb)

            for si_ in range(nseg):
                cp("pool", xTo[:, :, si_ * n:(si_ + 1) * n], xTb[:, :, si_ * (3 + n) + 3:(si_ + 1) * (3 + n)], [xTb], [xTo])

            def xo(kc):
                return xTo[:, kc, :]

            def ev_pre(j, pb):
                cp("act", pre[:, j, :ncol], pb[:, :ncol], [pb], [presub[j]])
            NCH = 24 if (full or (outs is not None and "conv" in outs)) else 20
            yield from feat(C_XBC, 128, NCH, ev_pre)
            if kind == 1:
                for si in range(nseg):
                    dma(pre[:, :, si * (3 + n):si * (3 + n) + 3], aux["convst"][:, :, si, :], writes=presub)
            for j0 in range(0, NCH, 4):
                js = list(range(j0, min(NCH, j0 + 4)))
                for i in range(4):
                    for j in js:
                        pj = pre[:, j, :ncol].rearrange("p (s c) -> p s c", c=3 + n)
                        acc = acc4[j - j0]
                        av = acc[:, :].rearrange("p (s c) -> p s c", c=n)
                        if i == 0:
                            act(av, pj[:, :, 0:n], AF.Identity, [presub[j], cw, cb], [acc], scale=cw[:, j, 0:1], bias=cb[:, j:j + 1])
                        else:
                            stt(av, pj[:, :, i:i + n], cw[:, j, i:i + 1], av, ALU.mult, ALU.add, [presub[j], acc, cw], [acc])
                for j in js:
                    act(xc[:, j, :], acc4[j - j0][:, :], AF.Silu, [acc4[j - j0]], [xcsub[j]])
                yield
            if outs is not None and "conv" in outs:
                co = outs["conv"]
                if kind == 0:
                    dma(co[:], pre[:, :, ncol - 3:ncol], reads=presub, writes=[co])
                else:
                    for si in range(nseg):
                        dma(co[:, :, si, :], pre[:, :, si * (3 + n) + 32:si * (3 + n) + 35], reads=presub, writes=[co])
            S = small

            def ev_dt(pb):
                tt("dve", S["xb"][:], pb[:, :32], dtbs[:], ALU.add, [pb, dtbs], [S["xb"]])
            tokmm(w_in, C_DT, 32, xo, 8, ev_dt, [xTo])
            stt(S["ab"][:], S["xb"][:], -1.0, S["xb"][:], ALU.mult, ALU.max, [S["xb"]], [S["ab"]])
            act(S["e"][:], S["ab"][:], AF.Exp, [S["ab"]], [S["e"]], scale=-1.0)
            act(S["l"][:], S["e"][:], AF.Ln, [S["e"]], [S["l"]], bias=1.0)
            stt(S["dt"][:], S["xb"][:], 0.0, S["l"][:], ALU.max, ALU.add, [S["xb"], S["l"]], [S["dt"]])
            if kind == 1:
                ts("dve", S["dt"][:], S["dt"][:], rvs[:, 0:1], None, ALU.mult, None, [S["dt"], rvs], [S["dt"]])
            tt("dve", S["a"][:], S["dt"][:], Aneg[:], ALU.mult, [S["dt"], Aneg], [S["a"]])
            pb = bank()
            mm(pb[:, 0:32], CM[0][:], S["a"][:], True, True, [CM[0], S["a"]], [pb])
            mm(pb[:, 32:64], CM[1][:], S["a"][:], True, True, [CM[1], S["a"]], [pb])
            cp("dve", S["acum"][:], pb[:, 0:32], [pb], [S["acum"]])
            ts("dve", S["nacum"][:], pb[:, 0:32], -1.0, None, ALU.mult, None, [pb], [S["nacum"]])
            tt("dve", S["tmp"][:], pb[:, 32:64], S["acum"][:], ALU.subtract, [pb, S["acum"]], [S["tmp"]])
            act(S["w"][:], S["tmp"][:], AF.Exp, [S["tmp"]], [S["w"]])
            tt("dve", S["w"][:], S["w"][:], S["dt"][:], ALU.mult, [S["w"], S["dt"]], [S["w"]])
            act(S["eA"][:], S["acum"][:], AF.Exp, [S["acum"]], [S["eA"]])

            def ev_kv(pb):
                cp("act", kvf[:, 0:512], pb[:, 0:512], [pb], [kvf])
            tokmm(w_in, C_K, 512, xo, 8, ev_kv, [xTo])

            def ev_ki(pb):
                cp("act", kvf[:, 512:576], pb[:, 0:64], [pb], [kvf])
            tokmm(w_in, C_KI, 64, xo, 8, ev_ki, [xTo])
            op("pool", lambda e: e.memset(v1[:], 1.0), writes=[v1])
            cp("pool", v1[:, :, 0:64], kvf[:, 256:512].rearrange("p (k d) -> p k d", d=64), [kvf], [v1])

            def ev_kt(j, pb):
                cp("act", ktb[:, j, :].rearrange("p (s c) -> p s c", c=n), own(pb[:64, :ncol]), [pb], [ktb])
            yield from feat(C_K, 64, 4, ev_kt)

            def ev_kit(j, pb):
                cp("act", ktb[:, 4, :].rearrange("p (s c) -> p s c", c=n), own(pb[:64, :ncol]), [pb], [ktb])
            yield from feat(C_KI, 64, 1, ev_kit)
            for g0 in range(0, 20, 8):
                ng = min(8, 20 - g0)
                for j in range(g0, g0 + ng):
                    tr(PTR[:, (j - g0) * 128:(j - g0 + 1) * 128], xc[:, j, :], [xcsub[j]], PTRS)
                if g0 < 16:
                    cp("act", xtok[:, g0 * 128:(g0 + ng) * 128], PTR[:, :ng * 128], PTRS, [xtok])
                else:
                    cp("act", btok[:, :], PTR[:, :512], PTRS, [btok])
            tt("dve", wx[:].rearrange("p (h d) -> p h d", d=64), xtok[:].rearrange("p (h d) -> p h d", d=64),
               bcast(S["w"][:].unsqueeze(2), [128, 32, 64]), ALU.mult, [xtok, S["w"]], [wx])
            for si, sg in enumerate(segs):
                p0 = sg["wpos"]
                if p0 is None:
                    continue
                rs = slice(si * n, (si + 1) * n)
                dma(sg["KT"][:, :, p0:p0 + n], ktb[:, 0:4, rs], reads=[ktb], writes=[sg["KT"]])
                dma(sg["kiT"][:, p0:p0 + n], ktb[:, 4, rs], reads=[ktb], writes=[sg["kiT"]])
                dma(sg["V"][p0:p0 + n, :], v1[rs, :, :].rearrange("p k d -> p (k d)"), reads=[v1], writes=[sg["V"]])
            if outs is not None and "k" in outs:
                ko, vo, kio, r0 = outs["k"], outs["v"], outs["ki"], outs["row0"]
                dma(ko[r0:r0 + 128, :], kvf[:, 0:256], reads=[kvf], writes=[ko])
                dma(vo[r0:r0 + 128, :], kvf[:, 256:512], reads=[kvf], writes=[vo])
                dma(kio[r0:r0 + 128, :], kvf[:, 512:576], reads=[kvf], writes=[kio])

            def state_update(si, Hbuf):
                rs = slice(si * n, (si + 1) * n)
                pbd = bank()
                sel = onesf if kind == 0 else None
                if kind == 0:
                    mm(pbd[:, 0:32], onesf[:], S["a"][:], True, True, [onesf, S["a"]], [pbd])
                else:
                    ts("dve", S["tmp"][:], S["a"][:], cm[1][1][:, si * n:si * n + 1], None, ALU.mult, None, [S["a"], cm[1][1]], [S["tmp"]])
                    mm(pbd[:, 0:32], onesf[:], S["tmp"][:], True, True, [onesf, S["tmp"]], [pbd])
                act(S["dec"][:], pbd[:, 0:32], AF.Exp, [pbd], [S["dec"]])
                pbs = [PB[3], PB[4], PB[5], PB[6]]
                for g in range(4):
                    mm(pbs[g][:, :], btok[rs, g * 128:(g + 1) * 128], wx[rs, g * 512:(g + 1) * 512], True, True, [btok, wx], [pbs[g]])
                tt("dve", Hbuf[:].rearrange("p (h d) -> p h d", d=64), Hbuf[:].rearrange("p (h d) -> p h d", d=64),
                   bcast(S["dec"][:].unsqueeze(2), [128, 32, 64]), ALU.mult, [Hbuf, S["dec"]], [Hbuf])
                for g in range(4):
                    tt("dve", Hbuf[:, g * 512:(g + 1) * 512], Hbuf[:, g * 512:(g + 1) * 512], pbs[g][:, :], ALU.add, [Hbuf, pbs[g]], [Hbuf])

            if not full:
                if blk_i == 0:
                    op("pool", lambda e: e.memset(Hown[:], 0.0), writes=[Hown])
                stt(Hown[:], H[:], oh4s[:, blk_i:blk_i + 1], Hown[:], ALU.mult, ALU.add, [H, oh4s, Hown], [Hown])
                state_update(0, H)
                return

            def ev_q(j, pb):
                op("act", lambda e: e.mul(QT[:, :].rearrange("p (s h c) -> p s h c", s=nseg, h=16)[:, :, j, :], own(pb[:64, :ncol]), 0.125), reads=[pb], writes=[QT])
            yield from feat(C_Q, 64, 16, ev_q)

            def ev_qi(j, pb):
                cp("act", qiT[:, j, :].rearrange("p (s c) -> p s c", c=n), own(pb[:64, :ncol]), [pb], [qiT])
            yield from feat(C_QI, 64, 8, ev_qi)

            def ev_wi(pb):
                ts("dve", wis[:], pb[:, 0:8], 1.0 / (8.0 * math.sqrt(8.0)), None, ALU.mult, None, [pb], [wis])
            tokmm(w_in, C_WI, 8, xo, 8, ev_wi, [xTo])
            for q4 in range(4):
                def ev_z(pb, q4=q4):
                    act(zs[:, q4 * 512:(q4 + 1) * 512], pb[:, :], AF.Silu, [pb], [zs])
                tokmm(w_in, C_Z + q4 * 512, 512, xo, 8, ev_z, [xTo])
            for q4 in range(4):
                def ev_g(pb, q4=q4):
                    act(gate[:, q4 * 512:(q4 + 1) * 512], pb[:, :], AF.Sigmoid, [pb], [gate])
                tokmm(w_in, C_G1 + q4 * 512, 512, xo, 8, ev_g, [xTo])

            import os
            SUB = int(os.environ.get("K_SUB", "9"))
            if SUB < 1:
                return
            for si, sg in enumerate(segs):
                rs = slice(si * n, (si + 1) * n)
                if kind == 1:
                    dma(H[:], aux["ssm_in"][si], writes=[H])
                    cp("pool", Hown[:], H[:], [H], [Hown])
                pbs = [PB[3], PB[4], PB[5], PB[6]]
                for g in range(4):
                    mm(pbs[g][rs, :], xc[:, 20 + g, rs], Hown[:, g * 512:(g + 1) * 512], True, True, [xcsub[20 + g], Hown], [pbs[g]])
                for g in range(4):
                    tt("dve", yo[rs, g * 512:(g + 1) * 512].rearrange("p (h d) -> p h d", d=64),
                       pbs[g][rs, :].rearrange("p (h d) -> p h d", d=64),
                       bcast(S["eA"][rs, g * 8:(g + 1) * 8].unsqueeze(2), [n, 8, 64]), ALU.mult, [pbs[g], S["eA"]], [yo])
                if kind == 1:
                    state_update(si, H)
                    dma(aux["ssm_out"][si], H[:], reads=[H], writes=[ssm_s])
            GT = PB[0]
            for g in range(4):
                mm(GT[:, g * 128:(g + 1) * 128], xc[:, 16 + g, :], xc[:, 20 + g, :], True, True, [xcsub[16 + g], xcsub[20 + g]], [GT])
            for hg in range(4):
                tt("dve", D8[:], bcast(identf[:].unsqueeze(1), [128, 8, 128]),
                   bcast(S["acum"][:, hg * 8:(hg + 1) * 8].unsqueeze(2), [128, 8, 128]), ALU.mult, [identf, S["acum"]], [D8])
                ex = [PB[1], PB[2]]
                for b2 in range(2):
                    mm(ex[b2][:, :], onesf[:], D8[:, b2 * 4:(b2 + 1) * 4, :].rearrange("p h q -> p (h q)"), True, False, [onesf, D8], [ex[b2]])
                    mm(ex[b2][:, :], ident[:], penrep[kind][:, :, :].rearrange("p h q -> p (h q)"), False, True,
                       [ident, penrep[kind]], [ex[b2]])
                for hh in range(8):
                    h = hg * 8 + hh
                    e_ = ex[hh // 4]
                    act(Dec[:, hh, :], e_[:, (hh % 4) * 128:(hh % 4 + 1) * 128], AF.Exp, [e_, S["nacum"]], [Dec], bias=S["nacum"][:, h:h + 1])
                    stt(MT[:, hh, :], Dec[:, hh, :], S["dt"][:, h:h + 1], GT[:, hg * 128:(hg + 1) * 128], ALU.mult, ALU.mult,
                        [Dec, S["dt"], GT], [MT])
                yd = PB[3 + hg % 2]
                for hh in range(8):
                    h = hg * 8 + hh
                    mm(yd[:, hh * 64:(hh + 1) * 64], MT[:, hh, :], xtok[:, h * 64:(h + 1) * 64], True, True, [MT, xtok], [yd])
                tt("dve", ysb[:, hg * 512:(hg + 1) * 512], yd[:, :], yo[:, hg * 512:(hg + 1) * 512], ALU.add, [yd, yo], [ysb])
            tt("dve", yo[:].rearrange("p (h d) -> p h d", d=64), xtok[:].rearrange("p (h d) -> p h d", d=64),
               bcast(dsks[:].unsqueeze(2), [128, 32, 64]), ALU.mult, [xtok, dsks], [yo])
            tt("dve", ysb[:], ysb[:], yo[:], ALU.add, [ysb, yo], [ysb])
            tt("dve", ysb[:], ysb[:], zs[:], ALU.mult, [ysb, zs], [ysb])
            for g in range(4):
                act(yo[:, g * 512:(g + 1) * 512], ysb[:, g * 512:(g + 1) * 512], AF.Square, [ysb], [yo, st4], accum_out=st4[:, g:g + 1])
            act(st4[:, 4:8], st4[:, 0:4], AF.Sqrt, [st4], [st4], scale=1.0 / 512.0, bias=1e-5)
            op("dve", lambda e: e.reciprocal(out=st4[:, 0:4], in_=st4[:, 4:8]), reads=[st4], writes=[st4])
            for g in range(4):
                ts("dve", ynb[:, g * 512:(g + 1) * 512], ysb[:, g * 512:(g + 1) * 512], st4[:, g:g + 1], None, ALU.mult, None, [ysb, st4], [ynb])
            for g0 in range(0, 16, 8):
                for j in range(g0, g0 + 8):
                    tr(PTR[:, (j - g0) * 128:(j - g0 + 1) * 128], ynb[:, j * 128:(j + 1) * 128], [ynb], PTRS)
                for j in range(g0, g0 + 8):
                    ts("dve", ynT[:, j, :], PTR[:, (j - g0) * 128:(j - g0 + 1) * 128], nws[:, j:j + 1], None, ALU.mult, None, PTRS + [nws], [ynT])
            for hf in range(2):
                def ev_y1(pb, hf=hf):
                    tt("dve", m1[:, hf * 512:(hf + 1) * 512], pb[:, :], gate[:, hf * 512:(hf + 1) * 512], ALU.mult, [pb, gate], [m1])
                tokmm(w_ssd, hf * 512, 512, lambda kc: ynT[:, kc, :], 16, ev_y1, [ynT])

            if SUB < 2:
                return
            nkc = segs[0]["nkc"]
            Stot = nkc * 128
            ti = [0]
            tiles = list(range(0, Stot, 512))
            grp = 2 if kind == 0 else 1
            for g0 in range(0, len(tiles), grp):
                gt = tiles[g0:g0 + grp]
                info = []
                for gi, t0 in enumerate(gt):
                    tw = min(512, Stot - t0)
                    kbs = []
                    for si, sg in enumerate(segs):
                        kb = kit[ti[0] % 2]
                        ti[0] += 1
                        dma(kb[:, :tw], sg["kiT"][:, t0:t0 + tw], reads=[sg["kiT"]], writes=[kb])
                        kbs.append(kb)
                    info.append((t0, tw, kbs, PB[1 + gi], rl[gi], scT[gi]))
                for ih in range(8):
                    for (t0, tw, kbs, pb, r_, scx) in info:
                        for si in range(nseg):
                            rs = slice(si * n, (si + 1) * n)
                            mm(pb[rs, :tw], qiT[:, ih, rs], kbs[si][:, :tw], True, True, [qiT, kbs[si]], [pb])
                        act(r_[:, :tw], pb[:, :tw], AF.Relu, [pb], [r_])
                        if ih == 0:
                            ts("dve", sc[:, t0:t0 + tw], r_[:, :tw], wis[:, 0:1], None, ALU.mult, None, [r_, wis], [scx, sc])
                        else:
                            stt(sc[:, t0:t0 + tw], r_[:, :tw], wis[:, ih:ih + 1], sc[:, t0:t0 + tw], ALU.mult, ALU.add, [r_, wis, scx], [scx])
            B_ = bis
            op("dve", lambda e: e.tensor_reduce(out=B_["B"][:], in_=sc[:, :Stot], axis=AX.X, op=ALU.max, apply_absolute_value=True),
               reads=[sc, scT[0], scT[1]], writes=[B_["B"]])
            npen = 4 if kind == 0 else 1
            for cc in range(npen):
                c = nkc - npen + cc
                tt("dve", sc[:, c * 128:(c + 1) * 128], sc[:, c * 128:(c + 1) * 128], vpen[kind][:, cc, :], ALU.add, [sc, vpen[kind]], [sc])
            ts("dve", B_["hi"][:], B_["B"][:], 1.0, None, ALU.add, None, [B_["B"]], [B_["hi"]])
            ts("dve", B_["lo"][:], B_["hi"][:], -1.0, None, ALU.mult, None, [B_["hi"]], [B_["lo"]])
            ts("dve", B_["d"][:], B_["hi"][:], 2.0, None, ALU.mult, None, [B_["hi"]], [B_["d"]])
            nj = (Stot + 4095) // 4096
            for it in range(NBIS):
                ck = 2.0 ** -(it + 1)
                ts("dve", B_["mid"][:], B_["d"][:], ck, B_["lo"][:, 0:1], ALU.mult, ALU.add, [B_["d"], B_["lo"]], [B_["mid"]])
                for j in range(nj):
                    w_ = min(4096, Stot - j * 4096)
                    op("dve", lambda e, j=j, w_=w_: e.tensor_scalar(out=junk[:, :w_], in0=sc[:, j * 4096:j * 4096 + w_], scalar1=B_["mid"][:, 0:1],
                                                                  scalar2=None, op0=ALU.is_ge, op1=ALU.add, accum_out=cnt4[:, j:j + 1]),
                       reads=[sc, B_["mid"]], writes=[junk, cnt4])
                if nj > 1:
                    op("dve", lambda e: e.tensor_reduce(out=B_["cnt"][:], in_=cnt4[:, :nj], axis=AX.X, op=ALU.add), reads=[cnt4], writes=[B_["cnt"]])
                    cn = B_["cnt"]
                else:
                    cn = cnt4
                ts("dve", B_["cond"][:], cn[:, 0:1], float(TOPK) - 0.5, ck, ALU.is_ge, ALU.mult, [cn], [B_["cond"]])
                stt(B_["lo"][:], B_["cond"][:], B_["d"][:, 0:1], B_["lo"][:], ALU.mult, ALU.add, [B_["cond"], B_["d"], B_["lo"]], [B_["lo"]])
                pump(3)
            flush()
            tau = B_["lo"]

            if SUB < 3:
                return
            OB = [PB[4], PB[5], PB[6]]
            for b3 in range(3):
                mm(OB[b3][:, :], zeros[:, 0:128], zeros[:, 0:512], True, False, [zeros], [OB[b3]])

            def ocol(h):
                return (h // 7), (h % 7) * 65
            tbl = 0 if kind == 0 else 1
            units = []
            for si, sg in enumerate(segs):
                for t0 in range(0, Stot, 512):
                    tw = min(512, Stot - t0)
                    for ch in range(tw // 128):
                        for half in range(2):
                            units.append((si, sg, t0, tw, ch, half))
            tstate = {}

            def s1(u):
                si, sg, t0, tw, ch, half = units[u]
                nch = tw // 128
                if ch == 0 and half == 0:
                    i2 = ti[0] % 2
                    ti[0] += 1
                    kt_, vt_, mb_ = ktt[i2], vtt[i2], mbt[i2]
                    dma(kt_[:, :, :tw], sg["KT"][:, :, t0:t0 + tw], reads=[sg["KT"]], writes=[kt_])
                    dma(vt_[:, :nch, :], sg["V"][t0:t0 + tw, :].rearrange("(c p) d -> p c d", p=128), reads=[sg["V"]], writes=[vt_])
                    ts("dve", mb_[:, :tw], sc[:, t0:t0 + tw], tau[:, 0:1], None, ALU.is_ge, None, [sc, tau], [mb_])
                    tstate[(si, t0)] = (kt_, vt_, mb_)
                kt_, vt_, mb_ = tstate[(si, t0)]
                c = t0 // 128 + ch
                near = c >= nkc - (5 if kind == 0 else 2)
                btv = bt[:].rearrange("p h q -> p (h q)")[:, 0:16 * n]
                bth = btH[half]
                if near:
                    ccp = c - (nkc - 5) if kind == 0 else c - (nkc - 2)
                    bi = ccp if kind == 0 else 5 + ccp
                    dma(btv[:, 8 * half * n:(8 * half + 8) * n], BTs[bi, :, 8 * half * n:(8 * half + 8) * n], reads=[BTs], writes=[bth, bt])
                stp = (PB[0], PB[1]) if u % 2 == 0 else (PB[2], PB[3])
                if half == 0:
                    slot = (u // 2) % 8
                    tr(PTR[:, slot * 128:(slot + 1) * 128], mb_[:, ch * 128:(ch + 1) * 128], [mb_], [PTRS[slot]])
                    cp("act", mTs[(u // 2) % 2][:, :], PTR[:, slot * 128:(slot + 1) * 128], [PTRS[slot]], [mTs[(u // 2) % 2]])
                for k2 in range(2):
                    kv = half * 2 + k2
                    pb = stp[k2]
                    outv = pb[:, :4 * n]
                    mm(outv, kt_[:, kv, ch * 128:(ch + 1) * 128], QT[:, si * 16 * n + 4 * kv * n:si * 16 * n + (4 * kv + 4) * n], True, not near, [kt_, QT], [pb])
                    if near:
                        mm(outv, antiI[:], btv[:, 4 * kv * n:(4 * kv + 4) * n], False, True, [antiI, bth], [pb])

            def s23(u):
                si, sg, t0, tw, ch, half = units[u]
                nch = tw // 128
                rs = slice(si * n, (si + 1) * n)
                kt_, vt_, mb_ = tstate[(si, t0)]
                stp = (PB[0], PB[1]) if u % 2 == 0 else (PB[2], PB[3])
                pt = PT[u % 2]
                for k2 in range(2):
                    act(pt[:, 4 * k2:4 * k2 + 4, 0:n], stp[k2][:, :4 * n].rearrange("p (h q) -> p h q", q=n), AF.Exp, [stp[k2]], [pt])
                slot = (u // 2) % 8
                mTb = mTs[(u // 2) % 2]
                mT = mTb[:, si * n:(si + 1) * n]
                tt("dve", pt[:, :, 0:n], pt[:, :, 0:n], bcast(mT.unsqueeze(1), [128, 8, n]), ALU.mult, [pt, mTb], [pt])
                for hh in range(8):
                    h = half * 8 + hh
                    kv = h // 4
                    b3, c0 = ocol(h)
                    last = (t0 + tw >= Stot) and ch == nch - 1
                    mm(OB[b3][rs, c0:c0 + 65], pt[:, hh, 0:n], vt_[:, ch, kv * 65:(kv + 1) * 65], False, last, [pt, vt_], [OB[b3]])

            s1(0)
            for u in range(len(units)):
                if u + 1 < len(units):
                    s1(u + 1)
                s23(u)
            for b3 in range(3):
                nh = 7 if b3 < 2 else 2
                ov = OB[b3][:, 0:nh * 65].rearrange("p (h d) -> p h d", d=65)
                op("dve", lambda e, ov=ov, b3=b3, nh=nh: e.reciprocal(out=rinv[:, b3 * 7:b3 * 7 + nh].unsqueeze(2), in_=ov[:, :, 64:65]),
                   reads=[OB[b3]], writes=[rinv])
                tt("dve", osb[:, b3 * 7 * 64:(b3 * 7 + nh) * 64].rearrange("p (h d) -> p h d", d=64), ov[:, :, 0:64],
                   bcast(rinv[:, b3 * 7:b3 * 7 + nh].unsqueeze(2), [128, nh, 64]), ALU.mult, [OB[b3], rinv], [osb])

            def transpose8(src, dst):
                for j in range(8):
                    tr(PTR[:, j * 128:(j + 1) * 128], src[:, j * 128:(j + 1) * 128], [src], PTRS)
                cp("act", dst[:].rearrange("p j q -> p (j q)"), PTR[:, :], PTRS, [dst])
            if SUB < 4:
                return
            transpose8(osb, tT)
            for hf in range(2):
                def ev_y2(pb, hf=hf):
                    tt("dve", hs[:, hf * 512:(hf + 1) * 512], pb[:, :],
                       gate[:, 1024 + hf * 512:1024 + (hf + 1) * 512], ALU.mult, [pb, gate], [hs])
                tokmm(w_att, hf * 512, 512, lambda kc: tT[:, kc, :], 8, ev_y2, [tT])
            tt("dve", mb16[:], m1[:], hs[:], ALU.add, [m1, hs], [mb16])
            transpose8(mb16, tT)
            dma(xrow[:], segs[0]["xrow"], writes=[xrow])

            def layer_norm(src, gi, dst, dstap, lnt):
                op("dve", lambda e: e.tensor_reduce(out=st4[:, 0:1], in_=src[:], axis=AX.X, op=ALU.add), reads=[src], writes=[st4])
                ts("dve", st4[:, 1:2], st4[:, 0:1], -1.0 / D, None, ALU.mult, None, [st4], [st4])
                ts("dve", src[:], src[:], st4[:, 1:2], None, ALU.add, None, [src, st4], [src])
                act(yo[:, 0:D], src[:], AF.Square, [src], [yo, st4], accum_out=st4[:, 2:3])
                act(st4[:, 3:4], st4[:, 2:3], AF.Sqrt, [st4], [st4], scale=1.0 / D, bias=1e-5)
                op("dve", lambda e: e.reciprocal(out=st4[:, 4:5], in_=st4[:, 3:4]), reads=[st4], writes=[st4])
                ts("dve", src[:], src[:], st4[:, 4:5], None, ALU.mult, None, [src, st4], [src])
                dma(lnt[:], lnp[gi], writes=[lnt])
                tt("dve", src[:], src[:], lnt[:], ALU.mult, [src, lnt], [src])
                dma(lnt[:], lnp[gi + 1], writes=[lnt])
                tt("dve", dstap, src[:], lnt[:], ALU.add, [src, lnt], [dst])

            for hf in range(2):
                def ev_mix(pb, hf=hf):
                    stt(m1[:, hf * 512:(hf + 1) * 512], xrow[:, hf * 512:(hf + 1) * 512], ALPHA, pb[:, :], ALU.mult, ALU.add, [xrow, pb], [m1])
                tokmm(w_o, hf * 512, 512, lambda kc: tT[:, kc, :], 8, ev_mix, [tT])
            layer_norm(m1, 0, hs, hs[:], View(ysb, ysb[:, 0:D]))
            cp("pool", mb16[:], hs[:], [hs], [mb16])
            transpose8(mb16, tT)
            for c4 in range(8):
                w = nextw()
                wload(w, w_up, 0, c4 * 512, 512)
                pb = bank()
                for j in range(4):
                    for kc in range(8):
                        mm(pb[:, j * 128:(j + 1) * 128], w[:, kc, j * 128:(j + 1) * 128], tT[:, kc, :], kc == 0, kc == 7, [w, tT], [pb])
                r_ = rl[c4 % 2]
                act(r_[:, :], pb[:, :], AF.Relu, [pb], [r_])
                tt("pool", fT[:, c4 * 4:(c4 + 1) * 4, :].rearrange("p j q -> p (j q)"), r_[:, :], r_[:, :], ALU.mult, [r_], [fT])
            for hf in range(2):
                def ev_dn(pb, hf=hf):
                    stt(m1[:, hf * 512:(hf + 1) * 512], hs[:, hf * 512:(hf + 1) * 512], ALPHA, pb[:, :], ALU.mult, ALU.add, [hs, pb], [m1])
                tokmm(w_dn, hf * 512, 512, lambda kc: fT[:, kc, :], 32, ev_dn, [fT])
            layer_norm(m1, 2, ysb, ysb[:, 0:D], View(yo, yo[:, 0:D]))
            dma(outs["y"], ysb[:, 0:D], reads=[ysb], writes=[outs["ybuf"]])

        op("pool", lambda e: e.memset(H[:], 0.0), writes=[H])
        pseg = dict(KT=KTp, V=Vp, kiT=kiTp)
        import os
        STAGE = int(os.environ.get("K_STAGE", "3"))
        def state_jobs(k):
            for i in range(4):
                blk = 4 * k + i
                sg = dict(pseg, wpos=blk * 128)
                outs = dict(k=k_p, v=v_p, ki=ki_p, row0=blk * 128)
                if blk == NB - 1:
                    outs["conv"] = conv_p
                yield from job(0, False, xT_p[:, :, blk * 128:blk * 128 + 131], [sg], blk_i=i, outs=outs)
                yield

        BGOV = os.environ.get("K_BG", "1") == "1"
        if STAGE >= 1:
            run(state_jobs(0))
        for k in range(NSLOT if STAGE >= 1 else 0):
            if k + 1 < NSLOT:
                if BGOV and STAGE >= 2:
                    bg[0] = state_jobs(k + 1)
            sg = dict(pseg, wpos=None, nkc=4 * k + 4, xrow=xrow_p[k])
            if STAGE >= 2:
                run(job(0, True, xTo_p[k], [sg], outs=dict(y=y_p[k], ybuf=y_p)))
                flush()
            if k + 1 < NSLOT and not (BGOV and STAGE >= 2):
                run(state_jobs(k + 1))
        dma(ssm_p[:], H[:], reads=[H], writes=[ssm_p])
        for jj in range(2 if STAGE >= 3 else 0):
            ssegs = [dict(KT=KTs[2 * jj + i], V=Vs[2 * jj + i], kiT=kiTs[2 * jj + i], wpos=PAST, nkc=NKC_S, xrow=xrow_s[jj]) for i in range(2)]
            run(job(1, True, xT_s[jj], ssegs, outs=dict(y=y_s[jj], ybuf=y_s, k=k_s, v=v_s, ki=ki_s, row0=128 * jj, conv=View(conv_s, conv_s[jj])),
                    aux=dict(convst=convst_s[jj], ssm_in=[ssmT_s[2 * jj + i] for i in range(2)], ssm_out=[ssm_s[2 * jj + i] for i in range(2)])))
        fw.finish()
    return nc


def _t5_bucket(rel):
    rel = np.asarray(rel, np.int64)
    half, max_exact = 16, 8
    n = np.abs(rel)
    lg = np.log(np.maximum(n, max_exact).astype(np.float32) / np.float32(max_exact)) / np.float32(math.log(128 / max_exact)) * np.float32(half - max_exact)
    large = max_exact + lg.astype(np.float32).astype(np.int32)
    large = np.minimum(large, half - 1)
    return np.where(rel > 0, half, 0) + np.where(n < max_exact, n, large)


_CACHE = {}


def kernel(x_prompt, x_sample, cache_k, cache_v, cache_kidx, state_ssm, state_conv, rel_bias,
           w_in, conv_w, conv_b, dt_bias, a_log, d_skip, ssd_norm_w, w_ssd_o, w_attn_o, w_out,
           ln1_g, ln1_b, w_up, w_down, ln2_g, ln2_b):
    f = lambda a: np.ascontiguousarray(np.asarray(a, dtype=np.float32))
    x_prompt, x_sample = f(x_prompt), f(x_sample)
    BATCH, SEQ, _ = x_prompt.shape
    DECB, DSEQ, _ = x_sample.shape
    PAST = cache_k.shape[2]
    assert BATCH == 2 and DECB == 32 and DSEQ == 32
    NB = SEQ // 128
    NSLOT = NB // 4
    key = (SEQ, PAST)
    if key not in _CACHE:
        _CACHE[key] = build(SEQ, PAST)
    nc = _CACHE[key]
    cache_k, cache_v, cache_kidx, state_ssm, state_conv = f(cache_k), f(cache_v), f(cache_kidx), f(state_ssm), f(state_conv)
    shared = {
        "w_in": f(w_in[0]), "w_ssd": f(w_ssd_o[0]), "w_att": f(w_attn_o[0]), "w_o": f(w_out[0]),
        "w_up": f(w_up[0]), "w_dn": f(w_down[0]),
        "convw": f(np.asarray(conv_w[0]).reshape(4, 24, 128).transpose(2, 1, 0)),
        "convb": f(np.asarray(conv_b[0]).reshape(24, 128).T),
        "dtb": f(np.broadcast_to(np.asarray(dt_bias[0])[None, :], (128, 32))),
        "alog": f(np.broadcast_to(np.asarray(a_log[0])[None, :], (128, 32))),
        "dsk": f(np.broadcast_to(np.asarray(d_skip[0])[None, :], (128, 32))),
        "normw": f(np.asarray(ssd_norm_w[0]).reshape(16, 128).T),
        "lnp": f(np.stack([np.broadcast_to(np.asarray(a[0])[None, :], (128, D)) for a in (ln1_g, ln1_b, ln2_g, ln2_b)])),
        "relb": f(rel_bias),
    }
    ar = np.arange(128)
    cm = np.zeros((2, 3, 128, 128), np.float32)
    cm[0, 0] = (ar[:, None] <= ar[None, :])
    cm[0, 1] = 1.0
    cm[0, 2] = np.where(ar[:, None] <= ar[None, :], 0.0, -1e4)
    same = (ar[:, None] // 64) == (ar[None, :] // 64)
    cm[1, 0] = same & (ar[:, None] <= ar[None, :])
    cm[1, 1] = same
    cm[1, 2] = np.where(same & (ar[:, None] <= ar[None, :]), 0.0, -1e4)
    idr = np.zeros((3, 128, 4, 128), np.float32)
    idr[0] = (ar[:, None, None] == ar[None, None, :])
    for si in range(2):
        idr[1 + si, :, :, :64] = (((ar[:, None, None] % 64) == np.arange(64)[None, None, :]) & ((ar[:, None, None] // 64) == si))
    shared["cmats"] = cm
    shared["idrep"] = idr
    xT = [np.concatenate([np.zeros((128, KC, 3), np.float32), x_prompt[b].T.reshape(KC, 128, SEQ).transpose(1, 0, 2)], axis=2) for b in range(2)]
    in_maps = []
    for c in range(8):
        b, r = c // 4, c % 4
        m = dict(shared)
        m["xT_p"] = np.ascontiguousarray(xT[b])
        own = [4 * k + r for k in range(NSLOT)]
        m["xrow_p"] = np.ascontiguousarray(np.stack([x_prompt[b, j * 128:(j + 1) * 128] for j in own]))
        m["xTo_p"] = np.ascontiguousarray(np.stack([xT[b][:, :, j * 128:j * 128 + 131] for j in own]))
        xs = x_sample[4 * c:4 * c + 4]
        xts = np.zeros((2, 128, KC, 2, 67), np.float32)
        xts[:, :, :, :, 3:35] = xs.transpose(2, 0, 1).reshape(KC, 128, 2, 2, 32).transpose(2, 1, 0, 3, 4)
        m["xT_s"] = xts.reshape(2, 128, KC, 134)
        xr = np.zeros((2, 2, 64, D), np.float32)
        xr[:, :, :32] = xs.reshape(2, 2, 32, D)
        m["xrow_s"] = xr.reshape(2, 128, D)
        sc_ = state_conv[0, 4 * c:4 * c + 4]
        m["convst_s"] = np.ascontiguousarray(sc_.reshape(2, 2, 3, 24, 128).transpose(0, 4, 3, 1, 2))
        m["rowvalid"] = ((ar % 64) < 32).astype(np.float32).reshape(128, 1)
        m["ssmT_s"] = np.ascontiguousarray(state_ssm[0, 4 * c:4 * c + 4].transpose(0, 3, 1, 2).reshape(4, 128, 2048))
        m["ckT_s"] = np.ascontiguousarray(cache_k[0, 4 * c:4 * c + 4].transpose(0, 3, 2, 1))
        m["cv_s"] = np.ascontiguousarray(cache_v[0, 4 * c:4 * c + 4].reshape(4, PAST, 256))
        m["ckiT_s"] = np.ascontiguousarray(cache_kidx[0, 4 * c:4 * c + 4].transpose(0, 2, 1))
        oh = np.zeros((2, 32, NREL), np.float32)
        ii = np.arange(NREL)
        for t, rr in enumerate((r, 0)):
            bk = _t5_bucket(128 * (3 - rr) + 127 - ii)
            oh[t, bk, ii] = 1.0
        m["onehot"] = oh
        vp = np.zeros((2, 128, 4, 128), np.float32)
        qpos = 128 * r + ar
        vend = (qpos // 64 + 1) * 64
        kpos = 128 * np.arange(4)[None, :, None] + ar[None, None, :]
        vp[0] = np.where(kpos < vend[:, None, None], 0.0, -1e30)
        vp[1, :, 0, 32:] = -1e30
        m["vispen"] = vp
        o4 = np.zeros((128, 4), np.float32)
        o4[:, r] = 1.0
        m["oh4"] = o4
        in_maps.append(m)
    res = run_bass_kernel_spmd(nc, in_maps, core_ids=list(range(8))).results
    y_prompt = np.zeros((2, SEQ, D), np.float32)
    y_sample = np.zeros((32, 32, D), np.float32)
    k_pr = np.zeros((1, 2, SEQ, 4, 64), np.float32)
    v_pr = np.zeros((1, 2, SEQ, 4, 64), np.float32)
    ki_pr = np.zeros((1, 2, SEQ, 64), np.float32)
    ssm_pr = np.zeros((1, 2, 32, 64, 128), np.float32)
    conv_pr = np.zeros((1, 2, 3, 3072), np.float32)
    k_sm = np.zeros((1, 32, 32, 4, 64), np.float32)
    v_sm = np.zeros((1, 32, 32, 4, 64), np.float32)
    ki_sm = np.zeros((1, 32, 32, 64), np.float32)
    ssm_sm = np.zeros((1, 32, 32, 64, 128), np.float32)
    conv_sm = np.zeros((1, 32, 3, 3072), np.float32)
    for c in range(8):
        b, r = c // 4, c % 4
        o = res[c]
        for k in range(NSLOT):
            j = 4 * k + r
            y_prompt[b, j * 128:(j + 1) * 128] = o["y_p"][k]
        y_sample[4 * c:4 * c + 4] = o["y_s"].reshape(4, 64, D)[:, :32]
        if r == 0:
            k_pr[0, b] = o["k_p"].reshape(SEQ, 4, 64)
            v_pr[0, b] = o["v_p"].reshape(SEQ, 4, 64)
            ki_pr[0, b] = o["ki_p"]
            ssm_pr[0, b] = o["ssm_p"].reshape(128, 32, 64).transpose(1, 2, 0)
            conv_pr[0, b] = o["conv_p"].transpose(2, 1, 0).reshape(3, 3072)
        k_sm[0, 4 * c:4 * c + 4] = o["k_s"].reshape(4, 64, 4, 64)[:, :32]
        v_sm[0, 4 * c:4 * c + 4] = o["v_s"].reshape(4, 64, 4, 64)[:, :32]
        ki_sm[0, 4 * c:4 * c + 4] = o["ki_s"].reshape(4, 64, 64)[:, :32]
        ssm_sm[0, 4 * c:4 * c + 4] = o["ssm_s"].reshape(4, 128, 32, 64).transpose(0, 2, 3, 1)
        conv_sm[0, 4 * c:4 * c + 4] = o["conv_s"].transpose(0, 3, 4, 2, 1).reshape(4, 3, 3072)
    return (y_prompt, y_sample, k_pr, v_pr, ki_pr, ssm_pr, conv_pr, k_sm, v_sm, ki_sm, ssm_sm, conv_sm)
```

```python
import contextlib
import math
import numpy as np
import concourse.bass as bass
import concourse.mybir as mybir
from concourse.bass_utils import run_bass_kernel_spmd

F32 = mybir.dt.float32
BF16 = mybir.dt.bfloat16
AF = mybir.ActivationFunctionType
ALU = mybir.AluOpType
AX = mybir.AxisListType

SAME_ENG_SYNC = True
import os
NOSYNC_ENG = set(os.environ.get('K_NOSYNC', 'pe').split(','))
STORES_ON_POOL = True
NDSEM = 32


class Buf:
    def __init__(self, t, name=""):
        self.t = t
        self.name = name
        self.w = None
        self.r = {}

    def __getitem__(self, k):
        return self.t[k]


class View:
    def __init__(self, parent, ap):
        self.p = parent
        self.t = ap

    def __getitem__(self, k):
        return self.t[k]

    @property
    def w(self):
        return self.p.w

    @w.setter
    def w(self, v):
        self.p.w = v

    @property
    def r(self):
        return self.p.r

    @r.setter
    def r(self, v):
        self.p.r = v


class FW:
    def __init__(self, nc, stack):
        self.nc = nc
        self.stack = stack
        self.engs = ["pe", "act", "dve", "pool", "sp"]
        self.sems = {}
        self.cnt = {}
        self.prog = {k: [] for k in self.engs}
        self.waited = {k: {} for k in self.engs}
        for k in self.engs:
            self.sems[k] = stack.enter_context(nc.semaphore("s_" + k))
            self.cnt[k] = 0
        self.dcnt = []
        for i in range(NDSEM):
            self.sems["d%d" % i] = stack.enter_context(nc.semaphore("d%d" % i))
            self.dcnt.append(0)
        self.dnext = 0
        self.nbuf = 0

    def sb(self, shape, dt, name=None):
        self.nbuf += 1
        name = name or ("sb%d" % self.nbuf)
        return Buf(self.stack.enter_context(self.nc.sbuf_tensor(name, list(shape), dt)), name)

    def ps(self, shape, dt, name=None):
        self.nbuf += 1
        name = name or ("ps%d" % self.nbuf)
        return Buf(self.stack.enter_context(self.nc.psum_tensor(name, list(shape), dt)), name)

    def _deps(self, e, reads, writes):
        need = {}

        def add(dep):
            if dep is None:
                return
            k, v = dep
            if k == e and (not SAME_ENG_SYNC or e in NOSYNC_ENG):
                return
            if need.get(k, 0) < v:
                need[k] = v

        for b in reads:
            add(b.w)
        for b in writes:
            add(b.w)
            for k, v in b.r.items():
                add((k, v))
        out = []
        wd = self.waited[e]
        for k, v in need.items():
            if wd.get(k, 0) >= v:
                continue
            wd[k] = v
            out.append((k, v))
        return out

    def op(self, e, fn, reads=(), writes=()):
        waits = self._deps(e, reads, writes)
        self.cnt[e] += 1
        v = self.cnt[e]
        self.prog[e].append((waits, fn, (e, 1)))
        for b in reads:
            if b.r.get(e, 0) < v:
                b.r[e] = v
        for b in writes:
            b.w = (e, v)
            b.r = {}

    def dma(self, out_ap, in_ap, reads=(), writes=(), q=None):
        if q is None:
            q = "pool" if (STORES_ON_POOL and "DRam" in type(out_ap.tensor).__name__) else "sp"
        i = self.dnext
        self.dnext = (self.dnext + 1) % NDSEM
        key = "d%d" % i
        waits = self._deps(q, reads, writes)
        prev = self.dcnt[i]
        if prev > 0 and self.waited[q].get(key, 0) < prev:
            self.waited[q][key] = prev
            waits.append((key, prev))
        self.dcnt[i] += 16
        v = self.dcnt[i]
        self.prog[q].append((waits, lambda eng: eng.dma_start(out=out_ap, in_=in_ap), (key, 16)))
        for b in reads:
            if b.r.get(key, 0) < v:
                b.r[key] = v
        for b in writes:
            b.w = (key, v)
            b.r = {}

    def finish(self):
        waits = []
        for i in range(NDSEM):
            key = "d%d" % i
            if self.dcnt[i] > 0 and self.waited["sp"].get(key, 0) < self.dcnt[i]:
                waits.append((key, self.dcnt[i]))
        self.prog["sp"].append((waits, None, None))
        nc = self.nc
        with nc.Block() as block:
            for ename, deco in (("sp", block.sync), ("pe", block.tensor), ("act", block.scalar),
                                ("dve", block.vector), ("pool", block.gpsimd)):
                prog = self.prog[ename]

                def body(eng, prog=prog):
                    for waits, fn, inc in prog:
                        for k, v in waits:
                            eng.wait_ge(self.sems[k], v)
                        if fn is None:
                            continue
                        ins = fn(eng)
                        ins.then_inc(self.sems[inc[0]], inc[1])

                deco(body)


D = 1024
KC = 8
NW = 9320
C_Z, C_XBC, C_DT, C_Q, C_K, C_V, C_QI, C_KI, C_WI, C_G1, C_G2 = 0, 2048, 5120, 5152, 6176, 6432, 6688, 7200, 7264, 7272, 8296
NREL = 768
TOPK = 256
ALPHA = 2.0 ** 0.25
NBIS = 25
NEG = -30000.0


def bcast(ap, shape):
    return ap.to_broadcast(list(shape))


def build(SEQ, PAST):
    NB = SEQ // 128
    NSLOT = NB // 4
    SPS = PAST + 128
    NKC_S = PAST // 128 + 1
    nc = bass.Bass("TRN2", target_bir_lowering=False)

    def din(name, shape, dt=F32):
        return Buf(nc.dram_tensor(name, list(shape), dt, kind="ExternalInput").ap(), name)

    def dout(name, shape, dt=F32):
        return Buf(nc.dram_tensor(name, list(shape), dt, kind="ExternalOutput").ap(), name)

    def dint(name, shape, dt):
        return Buf(nc.dram_tensor(name, list(shape), dt, kind="Internal").ap(), name)

    xT_p = din("xT_p", [128, KC, 3 + SEQ])
    xrow_p = din("xrow_p", [NSLOT, 128, D])
    xTo_p = din("xTo_p", [NSLOT, 128, KC, 131])
    xT_s = din("xT_s", [2, 128, KC, 134])
    xrow_s = din("xrow_s", [2, 128, D])
    convst_s = din("convst_s", [2, 128, 24, 2, 3])
    rowvalid = din("rowvalid", [128, 1])
    ssmT_s = din("ssmT_s", [4, 128, 2048])
    ckT_s = din("ckT_s", [4, 64, 4, PAST])
    cv_s = din("cv_s", [4, PAST, 256])
    ckiT_s = din("ckiT_s", [4, 64, PAST])
    w_in = din("w_in", [D, NW])
    w_ssd = din("w_ssd", [2048, D])
    w_att = din("w_att", [D, D])
    w_o = din("w_o", [D, D])
    w_up = din("w_up", [D, 4096])
    w_dn = din("w_dn", [4096, D])
    convw = din("convw", [128, 24, 4])
    convb = din("convb", [128, 24])
    dtb = din("dtb", [128, 32])
    alog = din("alog", [128, 32])
    dsk = din("dsk", [128, 32])
    normw = din("normw", [128, 16])
    lnp = din("lnp", [4, 128, D])
    relb = din("relb", [32, 16])
    onehot = din("onehot", [2, 32, NREL])
    vispen = din("vispen", [2, 128, 4, 128])
    oh4 = din("oh4", [128, 4])
    cmats = din("cmats", [2, 3, 128, 128])
    idrep_in = din("idrep", [3, 128, 4, 128])
    y_p = dout("y_p", [NSLOT, 128, D])
    y_s = dout("y_s", [2, 128, D])
    k_p = dout("k_p", [SEQ, 256])
    v_p = dout("v_p", [SEQ, 256])
    ki_p = dout("ki_p", [SEQ, 64])
    ssm_p = dout("ssm_p", [128, 2048])
    conv_p = dout("conv_p", [128, 24, 3])
    k_s = dout("k_s", [256, 256])
    v_s = dout("v_s", [256, 256])
    ki_s = dout("ki_s", [256, 64])
    ssm_s = dout("ssm_s", [4, 128, 2048])
    conv_s = dout("conv_s", [2, 128, 24, 2, 3])
    NT = 96
    wtiles = dint("wtiles", [NT, 128, 4096], BF16)
    BTs = dint("BTs", [7, 128, 2048], BF16)
    KTp = dint("KTp", [64, 4, SEQ], BF16)
    Vp = dint("Vp", [SEQ, 260], BF16)
    kiTp = dint("kiTp", [64, SEQ], BF16)
    KTs = [dint("KTs%d" % i, [64, 4, SPS], BF16) for i in range(4)]
    Vs = [dint("Vs%d" % i, [SPS, 260], BF16) for i in range(4)]
    kiTs = [dint("kiTs%d" % i, [64, SPS], BF16) for i in range(4)]
    TR = dint("TR", [2, 16, NREL], F32)

    with contextlib.ExitStack() as st:
        fw = FW(nc, st)
        op, dma = fw.op, fw.dma
        SMAX = max(SEQ, SPS)
        ident = fw.sb([128, 128], BF16, "ident")
        antiI = fw.sb([128, 128], BF16, "antiI")
        zeros = fw.sb([128, 512], BF16, "zeros")
        onesf = fw.sb([128, 128], F32, "onesf")
        cm = [[fw.sb([128, 128], F32, "cm%d%d" % (a, b)) for b in range(3)] for a in range(2)]
        penrep = [fw.sb([128, 4, 128], BF16, "penrep%d" % a) for a in range(2)]
        idrep = [fw.sb([128, 4, 128], BF16, "idrep0"), fw.sb([128, 4, 64], BF16, "idrep1"), fw.sb([128, 4, 64], BF16, "idrep2")]
        identf = fw.sb([128, 128], F32, "identf")
        vpen = [fw.sb([128, 4, 128], F32, "vpen%d" % a) for a in range(2)]
        oh4s = fw.sb([128, 4], F32, "oh4s")
        rvs = fw.sb([128, 1], F32, "rvs")
        cw = fw.sb([128, 24, 4], F32, "cw")
        cb = fw.sb([128, 24], F32, "cb")
        dtbs = fw.sb([128, 32], F32, "dtbs")
        Aneg = fw.sb([128, 32], F32, "Aneg")
        dsks = fw.sb([128, 32], F32, "dsks")
        nws = fw.sb([128, 16], F32, "nws")
        wt = [fw.sb([128, 8, 512], BF16, "wt%d" % i) for i in range(3)]
        wti = [0]
        xTb = fw.sb([128, KC, 140], BF16, "xTb")
        xTo = fw.sb([128, KC, 128], BF16, "xTo")
        pre = fw.sb([128, 24, 140], F32, "pre")
        xTf = pre
        acc2 = [fw.sb([128, 128], F32, "acc%d" % i) for i in range(2)]
        xc = fw.sb([128, 24, 128], BF16, "xc")
        xtok = fw.sb([128, 2048], BF16, "xtok")
        btok = fw.sb([128, 512], BF16, "btok")
        wx = fw.sb([128, 2048], BF16, "wx")
        small = {n: fw.sb([128, 32], F32, "sm_" + n) for n in
                 ["xb", "ab", "e", "l", "dt", "a", "acum", "nacum", "atot", "w", "eA", "dec", "tmp"]}
        H = fw.sb([128, 2048], F32, "H")
        Hown = fw.sb([128, 2048], BF16, "Hown")
        kvf = fw.sb([128, 576], F32, "kvf")
        v1 = fw.sb([128, 4, 65], BF16, "v1")
        ktb = fw.sb([64, 5, 128], BF16, "ktb")
        QT = fw.sb([64, 2048], BF16, "QT")
        qiT = fw.sb([64, 8, 128], BF16, "qiT")
        wis = fw.sb([128, 8], F32, "wis")
        zg = fw.sb([128, 4096], BF16, "zg")
        yo = fw.sb([128, 2048], F32, "yo")
        ysb = fw.sb([128, 2048], F32, "ysb")
        D8 = fw.sb([128, 8, 128], F32, "D8")
        ynb = wx
        st4 = fw.sb([128, 8], F32, "st4")
        sc = fw.sb([128, max(SMAX, 4096)], F32, "sc")
        junk = fw.sb([128, 2048], BF16, "junk")
        rl = [Buf(D8[:, 0:4, :].rearrange("p h q -> p (h q)"), "rlA"), Buf(D8[:, 4:8, :].rearrange("p h q -> p (h q)"), "rlB")]
        acc4 = [Buf(D8[:, i, :], "acc4_%d" % i) for i in range(4)]
        scT = [Buf(sc.t, "scA"), Buf(sc.t, "scB")]
        bis = {n: fw.sb([128, 1], F32, "bis_" + n) for n in ["lo", "hi", "mid", "cnt", "cond", "ncond", "d", "e", "B"]}
        cnt4 = fw.sb([128, 8], F32, "cnt4")
        mbt = [fw.sb([128, 512], BF16, "mbt0"), btok]
        kit = [fw.sb([64, 512], BF16, "kit%d" % i) for i in range(2)]
        ktt = [fw.sb([64, 4, 512], BF16, "ktt0"), View(wx, wx[0:64, :].rearrange("p (k s) -> p k s", k=4))]
        vtt = [fw.sb([128, 4, 260], BF16, "vtt0"), View(xTb, xTb[:, :, :].rearrange("p k c -> p (k c)")[:, 0:1040].rearrange("p (c d) -> p c d", c=4))]
        bt = fw.sb([128, 16, 128], BF16, "bt")
        PT = [fw.sb([128, 8, 128], BF16, "PT0"), xTo]
        MT = PT[0]
        Dec = View(bt, bt[:, 0:8, :])
        ynT = bt
        rinv = fw.sb([128, 16], F32, "rinv")
        rb = fw.sb([32, 16], F32, "rb")
        cfar = fw.sb([16, 1], F32, "cfar")
        stg = [View(sc, sc[:, 0:2048]), View(sc, sc[:, 2048:4096])]
        stgb = [View(junk, junk[:, 0:2048]), View(junk, junk[:, 0:2048])]
        zs = View(zg, zg[:, 0:2048])
        gate = View(zg, zg[:, 2048:4096])
        fT = View(zg, zg[:, :].rearrange("p (j q) -> p j q", q=128))
        btf = View(yo, yo[:, :].rearrange("p (h q) -> p h q", q=128))
        tT = View(xtok, xtok[:, 0:1024].rearrange("p (j q) -> p j q", q=128))
        mb16 = View(xtok, xtok[:, 1024:2048])
        osb = mb16
        m1 = View(ysb, ysb[:, D:2 * D])
        xrow = View(yo, yo[:, 0:D])
        trs = View(sc, sc[0:16, 0:NREL])
        ohs = View(sc, sc[0:32, 1024:1024 + NREL])
        hs = View(yo, yo[:, D:2 * D])
        presub = [Buf(pre.t, "pre%d" % j) for j in range(24)]
        xcsub = [Buf(xc.t, "xc%d" % j) for j in range(24)]
        PB = [fw.ps([128, 512], F32, "pb%d" % i) for i in range(7)]
        PTR = fw.ps([128, 1024], BF16, "ptr")
        PTRS = [Buf(PTR.t, "ptr%d" % i) for i in range(8)]
        btH = [Buf(bt.t, "btH0"), Buf(bt.t, "btH1")]
        mTs = [fw.sb([128, 128], BF16, "mTs%d" % i) for i in range(2)]

        def nextw():
            wti[0] = (wti[0] + 1) % 3
            return wt[wti[0]]

        def mm(out, lhsT, rhs, start, stop, reads, writes):
            op("pe", lambda e: e.matmul(out, lhsT=lhsT, rhs=rhs, start=start, stop=stop, skip_group_check=True),
               reads=reads, writes=writes)

        def tr(out, in_, reads, writes, idn=None):
            n = in_.shape[0]
            op("pe", lambda e: e.transpose(out, in_, ident[:n, :n]), reads=list(reads) + [ident], writes=writes)

        def act(out, in_, func, reads, writes, **kw):
            op("act", lambda e: e.activation(out=out, in_=in_, func=func, **kw), reads=reads, writes=writes)

        def tt(e_, out, in0, in1, o, reads, writes):
            op(e_, lambda e: e.tensor_tensor(out=out, in0=in0, in1=in1, op=o), reads=reads, writes=writes)

        def ts(e_, out, in0, s1, s2, op0, op1, reads, writes, **kw):
            if op1 is None:
                op(e_, lambda e: e.tensor_scalar(out=out, in0=in0, scalar1=s1, scalar2=None, op0=op0, **kw), reads=reads, writes=writes)
            else:
                op(e_, lambda e: e.tensor_scalar(out=out, in0=in0, scalar1=s1, scalar2=s2, op0=op0, op1=op1, **kw), reads=reads, writes=writes)

        def stt(out, in0, scalar, in1, op0, op1, reads, writes):
            op("dve", lambda e: e.scalar_tensor_tensor(out=out, in0=in0, scalar=scalar, in1=in1, op0=op0, op1=op1),
               reads=reads, writes=writes)

        def cp(e_, out, in_, reads, writes):
            if e_ == "act":
                act(out, in_, AF.Copy, reads, writes)
            else:
                op(e_, lambda e: e.tensor_copy(out=out, in_=in_), reads=reads, writes=writes)

        op("pool", lambda e: e.memset(ident[:], 0.0), writes=[ident])
        op("pool", lambda e: e.affine_select(out=ident[:], in_=ident[:], pattern=[[-1, 128]], compare_op=ALU.not_equal,
                                             fill=1.0, base=0, channel_multiplier=1), reads=[ident], writes=[ident])
        op("pool", lambda e: e.memset(antiI[:], 0.0), writes=[antiI])
        op("pool", lambda e: e.affine_select(out=antiI[:], in_=antiI[:], pattern=[[1, 128]], compare_op=ALU.not_equal,
                                             fill=1.0, base=-127, channel_multiplier=1), reads=[antiI], writes=[antiI])
        op("pool", lambda e: e.memset(zeros[:], 0.0), writes=[zeros])
        op("pool", lambda e: e.memset(onesf[:], 1.0), writes=[onesf])
        cp("dve", identf[:], ident[:], [ident], [identf])
        for a in range(2):
            for b in range(3):
                dma(cm[a][b][:], cmats[a, b], writes=[cm[a][b]])
            cp("dve", penrep[a][:], bcast(cm[a][2][:].unsqueeze(1), [128, 4, 128]), [cm[a][2]], [penrep[a]])
        for a in range(3):
            dma(stg[0][:, 0:512], idrep_in[a].rearrange("p g q -> p (g q)"), writes=[stg[0]])
            cp("dve", idrep[a][:], stg[0][:, 0:512].rearrange("p (g q) -> p g q", g=4)[:, :, 0:(128 if a == 0 else 64)], [stg[0]], [idrep[a]])
        for a in range(2):
            dma(vpen[a][:], vispen[a], writes=[vpen[a]])
        dma(oh4s[:], oh4[:], writes=[oh4s])
        dma(rvs[:], rowvalid[:], writes=[rvs])
        dma(cw[:], convw[:], writes=[cw])
        dma(cb[:], convb[:], writes=[cb])
        dma(dtbs[:], dtb[:], writes=[dtbs])
        dma(Aneg[:], alog[:], writes=[Aneg])
        act(Aneg[:], Aneg[:], AF.Exp, [Aneg], [Aneg])
        ts("dve", Aneg[:], Aneg[:], -1.0, None, ALU.mult, None, [Aneg], [Aneg])
        dma(dsks[:], dsk[:], writes=[dsks])
        dma(nws[:], normw[:], writes=[nws])
        dma(rb[:], relb[:], writes=[rb])
        dma(cfar[:], relb[15:16, :].rearrange("o h -> h o"), writes=[cfar])
        for t in range(2):
            dma(ohs[:], onehot[t], writes=[ohs])
            for hlf in range(2):
                mm(PB[0][:16, 0:384], rb[:], ohs[:, hlf * 384:(hlf + 1) * 384], True, True, [rb, ohs], [PB[0]])
                ts("dve", trs[:, hlf * 384:(hlf + 1) * 384], PB[0][:16, 0:384], cfar[:, 0:1], None, ALU.subtract, None,
                   [PB[0], cfar], [trs])
            dma(TR[t], trs[:], reads=[trs], writes=[TR])

        ci = [0]

        wtab = {}

        def wload(w, src, k0, c0, ncols):
            key = (src.name, k0, c0, ncols)
            if key not in wtab:
                idx = len(wtab)
                assert idx < NT
                wtab[key] = idx
                for hf_ in range(2):
                    dma(sc[:, 0:4 * ncols].rearrange("p (k c) -> p k c", k=4),
                        src[(k0 + 4 * hf_) * 128:(k0 + 4 * hf_ + 4) * 128, c0:c0 + ncols].rearrange("(k p) c -> p k c", p=128), writes=[sc])
                    cp("act", junk[:, 0:4 * ncols], sc[:, 0:4 * ncols], [sc], [junk])
                    dma(wtiles[idx, :, 4 * hf_ * ncols:(4 * hf_ + 4) * ncols], junk[:, 0:4 * ncols], reads=[junk], writes=[wtiles])
            idx = wtab[key]
            if ncols == 512:
                dma(w[:, :, :].rearrange("p k c -> p (k c)"), wtiles[idx, :, :], reads=[wtiles], writes=[w])
            else:
                dma(w[:, :, :ncols], wtiles[idx, :, 0:8 * ncols].rearrange("p (k c) -> p k c", k=8), reads=[wtiles], writes=[w])

        for bi in range(7):
            tb_, cc_, n_ = (0, bi, 128) if bi < 5 else (1, bi - 5, 64)
            src = bass.AP(tensor=TR.t.tensor, offset=tb_ * 16 * NREL + 128 * (4 - cc_), ap=[[1, 128], [NREL, 16], [1, n_]])
            dma(sc[:, 0:16 * n_].rearrange("p (h q) -> p h q", q=n_), src, reads=[TR], writes=[sc])
            cp("act", junk[:, 0:16 * n_], sc[:, 0:16 * n_], [sc], [junk])
            dma(BTs[bi, :, 0:16 * n_], junk[:, 0:16 * n_], reads=[junk], writes=[BTs])

        for s_ in range(4):
            for p0 in range(0, PAST, 512):
                i = ci[0] % 2
                ci[0] += 1
                dma(stg[i][:64, :].rearrange("p (k s) -> p k s", k=4), ckT_s[s_, :, :, p0:p0 + 512], writes=[stg[i]])
                cp("act", stgb[i][:64, :], stg[i][:64, :], [stg[i]], [stgb[i]])
                dma(KTs[s_][:, :, p0:p0 + 512], stgb[i][:64, :].rearrange("p (k s) -> p k s", k=4), reads=[stgb[i]], writes=[KTs[s_]])
                i = ci[0] % 2
                ci[0] += 1
                dma(stg[i][:, 0:1024].rearrange("p (c d) -> p c d", c=4),
                    cv_s[s_, p0:p0 + 512, :].rearrange("(c p) d -> p c d", p=128), writes=[stg[i]])
                op("pool", lambda e, i=i: e.memset(stgb[i][:, 0:1040], 1.0), writes=[stgb[i]])
                cp("dve", stgb[i][:, 0:1040].rearrange("p (c k d) -> p c k d", c=4, k=4)[:, :, :, 0:64],
                   stg[i][:, 0:1024].rearrange("p (c k d) -> p c k d", c=4, k=4), [stg[i]], [stgb[i]])
                dma(Vs[s_][p0:p0 + 512, :].rearrange("(c p) d -> p c d", p=128),
                    stgb[i][:, 0:1040].rearrange("p (c d) -> p c d", c=4), reads=[stgb[i]], writes=[Vs[s_]])
                i = ci[0] % 2
                ci[0] += 1
                dma(stg[i][:64, 0:512], ckiT_s[s_, :, p0:p0 + 512], writes=[stg[i]])
                cp("act", stgb[i][:64, 0:512], stg[i][:64, 0:512], [stg[i]], [stgb[i]])
                dma(kiTs[s_][:, p0:p0 + 512], stgb[i][:64, 0:512], reads=[stgb[i]], writes=[kiTs[s_]])
            dma(KTs[s_][:, :, PAST:PAST + 128], zeros[:64, :].rearrange("p (k s) -> p k s", k=4), reads=[zeros], writes=[KTs[s_]])
            dma(Vs[s_][PAST:PAST + 128, :], zeros[:, 0:260], reads=[zeros], writes=[Vs[s_]])
            dma(kiTs[s_][:, PAST:PAST + 128], zeros[:64, 0:128], reads=[zeros], writes=[kiTs[s_]])

        pbi = [0]

        def bank():
            pbi[0] = (pbi[0] + 1) % 4
            return PB[pbi[0]]

        bg = [None]

        def pump(nsteps):
            for _ in range(nsteps):
                if bg[0] is None:
                    return
                try:
                    next(bg[0])
                except StopIteration:
                    bg[0] = None

        def flush():
            while bg[0] is not None:
                pump(1)

        def run(gen):
            for _ in gen:
                pass

        def job(kind, full, xT_src, segs, blk_i=None, outs=None, aux=None):
            nseg = 1 if kind == 0 else 2
            n = 128 // nseg
            ncol = nseg * (3 + n)
            CM = cm[kind]

            def own(t3):
                return t3.rearrange("p (s c) -> p s c", c=3 + n)[:, :, 3:]

            dma(xTf[:, 0:KC, :ncol], xT_src, writes=presub[0:KC])
            cp("act", xTb[:, :, :ncol], xTf[:, 0:KC, :ncol], presub[0:KC], [xTb])

            def feat(c0, M, nchunk, evac, rhs_cols=None):
                per = 512 // M
                for j0 in range(0, nchunk, per):
                    nj = min(per, nchunk - j0)
                    w = nextw()
                    wload(w, w_in, 0, c0 + j0 * M, nj * M)
                    for j in range(j0, j0 + nj):
                        pb = bank()
                        for kc in range(KC):
                            mm(pb[:M, :ncol], w[:, kc, (j - j0) * M:(j - j0 + 1) * M], xTb[:, kc, :ncol], kc == 0, kc == KC - 1,
                               [w, xTb], [pb])
                        evac(j, pb)
                    yield

            def tokmm(wd, c0, ncols, lhs, nkc, evac, reads):
                pb = bank()
                for k0 in range(0, nkc, 8):
                    w = nextw()
                    wload(w, wd, k0, c0, ncols)
                    for kc in range(8):
                        mm(pb[:, :ncols], lhs(k0 + kc), w[:, kc, :ncols], k0 + kc == 0, k0 + kc == nkc - 1, [w] + reads, [pb])
                evac(pb)

            for si_ in range(nseg):
                cp("pool", xTo[:, :, si_ * n:(si_ + 1) * n], xTb[:, :, si_ * (3 + n) + 3:(si_ + 1) * (3 + n)], [xTb], [xTo])

            def xo(kc):
                return xTo[:, kc, :]

            def ev_pre(j, pb):
                cp("act", pre[:, j, :ncol], pb[:, :ncol], [pb], [presub[j]])
            NCH = 24 if (full or (outs is not None and "conv" in outs)) else 20
            yield from feat(C_XBC, 128, NCH, ev_pre)
            if kind == 1:
                for si in range(nseg):
                    dma(pre[:, :, si * (3 + n):si * (3 + n) + 3], aux["convst"][:, :, si, :], writes=presub)
            for j0 in range(0, NCH, 4):
                js = list(range(j0, min(NCH, j0 + 4)))
                for i in range(4):
                    for j in js:
                        pj = pre[:, j, :ncol].rearrange("p (s c) -> p s c", c=3 + n)
                        acc = acc4[j - j0]
                        av = acc[:, :].rearrange("p (s c) -> p s c", c=n)
                        if i == 0:
                            act(av, pj[:, :, 0:n], AF.Identity, [presub[j], cw, cb], [acc], scale=cw[:, j, 0:1], bias=cb[:, j:j + 1])
                        else:
                            stt(av, pj[:, :, i:i + n], cw[:, j, i:i + 1], av, ALU.mult, ALU.add, [presub[j], acc, cw], [acc])
                for j in js:
                    act(xc[:, j, :], acc4[j - j0][:, :], AF.Silu, [acc4[j - j0]], [xcsub[j]])
                yield
            if outs is not None and "conv" in outs:
                co = outs["conv"]
                if kind == 0:
                    dma(co[:], pre[:, :, ncol - 3:ncol], reads=presub, writes=[co])
                else:
                    for si in range(nseg):
                        dma(co[:, :, si, :], pre[:, :, si * (3 + n) + 32:si * (3 + n) + 35], reads=presub, writes=[co])
            S = small

            def ev_dt(pb):
                tt("dve", S["xb"][:], pb[:, :32], dtbs[:], ALU.add, [pb, dtbs], [S["xb"]])
            tokmm(w_in, C_DT, 32, xo, 8, ev_dt, [xTo])
            stt(S["ab"][:], S["xb"][:], -1.0, S["xb"][:], ALU.mult, ALU.max, [S["xb"]], [S["ab"]])
            act(S["e"][:], S["ab"][:], AF.Exp, [S["ab"]], [S["e"]], scale=-1.0)
            act(S["l"][:], S["e"][:], AF.Ln, [S["e"]], [S["l"]], bias=1.0)
            stt(S["dt"][:], S["xb"][:], 0.0, S["l"][:], ALU.max, ALU.add, [S["xb"], S["l"]], [S["dt"]])
            if kind == 1:
                ts("dve", S["dt"][:], S["dt"][:], rvs[:, 0:1], None, ALU.mult, None, [S["dt"], rvs], [S["dt"]])
            tt("dve", S["a"][:], S["dt"][:], Aneg[:], ALU.mult, [S["dt"], Aneg], [S["a"]])
            pb = bank()
            mm(pb[:, 0:32], CM[0][:], S["a"][:], True, True, [CM[0], S["a"]], [pb])
            mm(pb[:, 32:64], CM[1][:], S["a"][:], True, True, [CM[1], S["a"]], [pb])
            cp("dve", S["acum"][:], pb[:, 0:32], [pb], [S["acum"]])
            ts("dve", S["nacum"][:], pb[:, 0:32], -1.0, None, ALU.mult, None, [pb], [S["nacum"]])
            tt("dve", S["tmp"][:], pb[:, 32:64], S["acum"][:], ALU.subtract, [pb, S["acum"]], [S["tmp"]])
            act(S["w"][:], S["tmp"][:], AF.Exp, [S["tmp"]], [S["w"]])
            tt("dve", S["w"][:], S["w"][:], S["dt"][:], ALU.mult, [S["w"], S["dt"]], [S["w"]])
            act(S["eA"][:], S["acum"][:], AF.Exp, [S["acum"]], [S["eA"]])

            def ev_kv(pb):
                cp("act", kvf[:, 0:512], pb[:, 0:512], [pb], [kvf])
            tokmm(w_in, C_K, 512, xo, 8, ev_kv, [xTo])

            def ev_ki(pb):
                cp("act", kvf[:, 512:576], pb[:, 0:64], [pb], [kvf])
            tokmm(w_in, C_KI, 64, xo, 8, ev_ki, [xTo])
            op("pool", lambda e: e.memset(v1[:], 1.0), writes=[v1])
            cp("pool", v1[:, :, 0:64], kvf[:, 256:512].rearrange("p (k d) -> p k d", d=64), [kvf], [v1])

            def ev_kt(j, pb):
                cp("act", ktb[:, j, :].rearrange("p (s c) -> p s c", c=n), own(pb[:64, :ncol]), [pb], [ktb])
            yield from feat(C_K, 64, 4, ev_kt)

            def ev_kit(j, pb):
                cp("act", ktb[:, 4, :].rearrange("p (s c) -> p s c", c=n), own(pb[:64, :ncol]), [pb], [ktb])
            yield from feat(C_KI, 64, 1, ev_kit)
            for g0 in range(0, 20, 8):
                ng = min(8, 20 - g0)
                for j in range(g0, g0 + ng):
                    tr(PTR[:, (j - g0) * 128:(j - g0 + 1) * 128], xc[:, j, :], [xcsub[j]], PTRS)
                if g0 < 16:
                    cp("act", xtok[:, g0 * 128:(g0 + ng) * 128], PTR[:, :ng * 128], PTRS, [xtok])
                else:
                    cp("act", btok[:, :], PTR[:, :512], PTRS, [btok])
            tt("dve", wx[:].rearrange("p (h d) -> p h d", d=64), xtok[:].rearrange("p (h d) -> p h d", d=64),
               bcast(S["w"][:].unsqueeze(2), [128, 32, 64]), ALU.mult, [xtok, S["w"]], [wx])
            for si, sg in enumerate(segs):
                p0 = sg["wpos"]
                if p0 is None:
                    continue
                rs = slice(si * n, (si + 1) * n)
                dma(sg["KT"][:, :, p0:p0 + n], ktb[:, 0:4, rs], reads=[ktb], writes=[sg["KT"]])
                dma(sg["kiT"][:, p0:p0 + n], ktb[:, 4, rs], reads=[ktb], writes=[sg["kiT"]])
                dma(sg["V"][p0:p0 + n, :], v1[rs, :, :].rearrange("p k d -> p (k d)"), reads=[v1], writes=[sg["V"]])
            if outs is not None and "k" in outs:
                ko, vo, kio, r0 = outs["k"], outs["v"], outs["ki"], outs["row0"]
                dma(ko[r0:r0 + 128, :], kvf[:, 0:256], reads=[kvf], writes=[ko])
                dma(vo[r0:r0 + 128, :], kvf[:, 256:512], reads=[kvf], writes=[vo])
                dma(kio[r0:r0 + 128, :], kvf[:, 512:576], reads=[kvf], writes=[kio])

            def state_update(si, Hbuf):
                rs = slice(si * n, (si + 1) * n)
                pbd = bank()
                sel = onesf if kind == 0 else None
                if kind == 0:
                    mm(pbd[:, 0:32], onesf[:], S["a"][:], True, True, [onesf, S["a"]], [pbd])
                else:
                    ts("dve", S["tmp"][:], S["a"][:], cm[1][1][:, si * n:si * n + 1], None, ALU.mult, None, [S["a"], cm[1][1]], [S["tmp"]])
                    mm(pbd[:, 0:32], onesf[:], S["tmp"][:], True, True, [onesf, S["tmp"]], [pbd])
                act(S["dec"][:], pbd[:, 0:32], AF.Exp, [pbd], [S["dec"]])
                pbs = [PB[3], PB[4], PB[5], PB[6]]
                for g in range(4):
                    mm(pbs[g][:, :], btok[rs, g * 128:(g + 1) * 128], wx[rs, g * 512:(g + 1) * 512], True, True, [btok, wx], [pbs[g]])
                tt("dve", Hbuf[:].rearrange("p (h d) -> p h d", d=64), Hbuf[:].rearrange("p (h d) -> p h d", d=64),
                   bcast(S["dec"][:].unsqueeze(2), [128, 32, 64]), ALU.mult, [Hbuf, S["dec"]], [Hbuf])
                for g in range(4):
                    tt("dve", Hbuf[:, g * 512:(g + 1) * 512], Hbuf[:, g * 512:(g + 1) * 512], pbs[g][:, :], ALU.add, [Hbuf, pbs[g]], [Hbuf])

            if not full:
                if blk_i == 0:
                    op("pool", lambda e: e.memset(Hown[:], 0.0), writes=[Hown])
                stt(Hown[:], H[:], oh4s[:, blk_i:blk_i + 1], Hown[:], ALU.mult, ALU.add, [H, oh4s, Hown], [Hown])
                state_update(0, H)
                return

            def ev_q(j, pb):
                op("act", lambda e: e.mul(QT[:, :].rearrange("p (s h c) -> p s h c", s=nseg, h=16)[:, :, j, :], own(pb[:64, :ncol]), 0.125), reads=[pb], writes=[QT])
            yield from feat(C_Q, 64, 16, ev_q)

            def ev_qi(j, pb):
                cp("act", qiT[:, j, :].rearrange("p (s c) -> p s c", c=n), own(pb[:64, :ncol]), [pb], [qiT])
            yield from feat(C_QI, 64, 8, ev_qi)

            def ev_wi(pb):
                ts("dve", wis[:], pb[:, 0:8], 1.0 / (8.0 * math.sqrt(8.0)), None, ALU.mult, None, [pb], [wis])
            tokmm(w_in, C_WI, 8, xo, 8, ev_wi, [xTo])
            for q4 in range(4):
                def ev_z(pb, q4=q4):
                    act(zs[:, q4 * 512:(q4 + 1) * 512], pb[:, :], AF.Silu, [pb], [zs])
                tokmm(w_in, C_Z + q4 * 512, 512, xo, 8, ev_z, [xTo])
            for q4 in range(4):
                def ev_g(pb, q4=q4):
                    act(gate[:, q4 * 512:(q4 + 1) * 512], pb[:, :], AF.Sigmoid, [pb], [gate])
                tokmm(w_in, C_G1 + q4 * 512, 512, xo, 8, ev_g, [xTo])

            import os
            SUB = int(os.environ.get("K_SUB", "9"))
            if SUB < 1:
                return
            for si, sg in enumerate(segs):
                rs = slice(si * n, (si + 1) * n)
                if kind == 1:
                    dma(H[:], aux["ssm_in"][si], writes=[H])
                    cp("pool", Hown[:], H[:], [H], [Hown])
                pbs = [PB[3], PB[4], PB[5], PB[6]]
                for g in range(4):
                    mm(pbs[g][rs, :], xc[:, 20 + g, rs], Hown[:, g * 512:(g + 1) * 512], True, True, [xcsub[20 + g], Hown], [pbs[g]])
                for g in range(4):
                    tt("dve", yo[rs, g * 512:(g + 1) * 512].rearrange("p (h d) -> p h d", d=64),
                       pbs[g][rs, :].rearrange("p (h d) -> p h d", d=64),
                       bcast(S["eA"][rs, g * 8:(g + 1) * 8].unsqueeze(2), [n, 8, 64]), ALU.mult, [pbs[g], S["eA"]], [yo])
                if kind == 1:
                    state_update(si, H)
                    dma(aux["ssm_out"][si], H[:], reads=[H], writes=[ssm_s])
            GT = PB[0]
            for g in range(4):
                mm(GT[:, g * 128:(g + 1) * 128], xc[:, 16 + g, :], xc[:, 20 + g, :], True, True, [xcsub[16 + g], xcsub[20 + g]], [GT])
            for hg in range(4):
                tt("dve", D8[:], bcast(identf[:].unsqueeze(1), [128, 8, 128]),
                   bcast(S["acum"][:, hg * 8:(hg + 1) * 8].unsqueeze(2), [128, 8, 128]), ALU.mult, [identf, S["acum"]], [D8])
                ex = [PB[1], PB[2]]
                for b2 in range(2):
                    mm(ex[b2][:, :], onesf[:], D8[:, b2 * 4:(b2 + 1) * 4, :].rearrange("p h q -> p (h q)"), True, False, [onesf, D8], [ex[b2]])
                    mm(ex[b2][:, :], ident[:], penrep[kind][:, :, :].rearrange("p h q -> p (h q)"), False, True,
                       [ident, penrep[kind]], [ex[b2]])
                for hh in range(8):
                    h = hg * 8 + hh
                    e_ = ex[hh // 4]
                    act(Dec[:, hh, :], e_[:, (hh % 4) * 128:(hh % 4 + 1) * 128], AF.Exp, [e_, S["nacum"]], [Dec], bias=S["nacum"][:, h:h + 1])
                    stt(MT[:, hh, :], Dec[:, hh, :], S["dt"][:, h:h + 1], GT[:, hg * 128:(hg + 1) * 128], ALU.mult, ALU.mult,
                        [Dec, S["dt"], GT], [MT])
                yd = PB[3 + hg % 2]
                for hh in range(8):
                    h = hg * 8 + hh
                    mm(yd[:, hh * 64:(hh + 1) * 64], MT[:, hh, :], xtok[:, h * 64:(h + 1) * 64], True, True, [MT, xtok], [yd])
                tt("dve", ysb[:, hg * 512:(hg + 1) * 512], yd[:, :], yo[:, hg * 512:(hg + 1) * 512], ALU.add, [yd, yo], [ysb])
            tt("dve", yo[:].rearrange("p (h d) -> p h d", d=64), xtok[:].rearrange("p (h d) -> p h d", d=64),
               bcast(dsks[:].unsqueeze(2), [128, 32, 64]), ALU.mult, [xtok, dsks], [yo])
            tt("dve", ysb[:], ysb[:], yo[:], ALU.add, [ysb, yo], [ysb])
            tt("dve", ysb[:], ysb[:], zs[:], ALU.mult, [ysb, zs], [ysb])
            for g in range(4):
                act(yo[:, g * 512:(g + 1) * 512], ysb[:, g * 512:(g + 1) * 512], AF.Square, [ysb], [yo, st4], accum_out=st4[:, g:g + 1])
            act(st4[:, 4:8], st4[:, 0:4], AF.Sqrt, [st4], [st4], scale=1.0 / 512.0, bias=1e-5)
            op("dve", lambda e: e.reciprocal(out=st4[:, 0:4], in_=st4[:, 4:8]), reads=[st4], writes=[st4])
            for g in range(4):
                ts("dve", ynb[:, g * 512:(g + 1) * 512], ysb[:, g * 512:(g + 1) * 512], st4[:, g:g + 1], None, ALU.mult, None, [ysb, st4], [ynb])
            for g0 in range(0, 16, 8):
                for j in range(g0, g0 + 8):
                    tr(PTR[:, (j - g0) * 128:(j - g0 + 1) * 128], ynb[:, j * 128:(j + 1) * 128], [ynb], PTRS)
                for j in range(g0, g0 + 8):
                    ts("dve", ynT[:, j, :], PTR[:, (j - g0) * 128:(j - g0 + 1) * 128], nws[:, j:j + 1], None, ALU.mult, None, PTRS + [nws], [ynT])
            for hf in range(2):
                def ev_y1(pb, hf=hf):
                    tt("dve", m1[:, hf * 512:(hf + 1) * 512], pb[:, :], gate[:, hf * 512:(hf + 1) * 512], ALU.mult, [pb, gate], [m1])
                tokmm(w_ssd, hf * 512, 512, lambda kc: ynT[:, kc, :], 16, ev_y1, [ynT])

            if SUB < 2:
                return
            nkc = segs[0]["nkc"]
            Stot = nkc * 128
            ti = [0]
            tiles = list(range(0, Stot, 512))
            grp = 2 if kind == 0 else 1
            for g0 in range(0, len(tiles), grp):
                gt = tiles[g0:g0 + grp]
                info = []
                for gi, t0 in enumerate(gt):
                    tw = min(512, Stot - t0)
                    kbs = []
                    for si, sg in enumerate(segs):
                        kb = kit[ti[0] % 2]
                        ti[0] += 1
                        dma(kb[:, :tw], sg["kiT"][:, t0:t0 + tw], reads=[sg["kiT"]], writes=[kb])
                        kbs.append(kb)
                    info.append((t0, tw, kbs, PB[1 + gi], rl[gi], scT[gi]))
                for ih in range(8):
                    for (t0, tw, kbs, pb, r_, scx) in info:
                        for si in range(nseg):
                            rs = slice(si * n, (si + 1) * n)
                            mm(pb[rs, :tw], qiT[:, ih, rs], kbs[si][:, :tw], True, True, [qiT, kbs[si]], [pb])
                        act(r_[:, :tw], pb[:, :tw], AF.Relu, [pb], [r_])
                        if ih == 0:
                            ts("dve", sc[:, t0:t0 + tw], r_[:, :tw], wis[:, 0:1], None, ALU.mult, None, [r_, wis], [scx, sc])
                        else:
                            stt(sc[:, t0:t0 + tw], r_[:, :tw], wis[:, ih:ih + 1], sc[:, t0:t0 + tw], ALU.mult, ALU.add, [r_, wis, scx], [scx])
            B_ = bis
            op("dve", lambda e: e.tensor_reduce(out=B_["B"][:], in_=sc[:, :Stot], axis=AX.X, op=ALU.max, apply_absolute_value=True),
               reads=[sc, scT[0], scT[1]], writes=[B_["B"]])
            npen = 4 if kind == 0 else 1
            for cc in range(npen):
                c = nkc - npen + cc
                tt("dve", sc[:, c * 128:(c + 1) * 128], sc[:, c * 128:(c + 1) * 128], vpen[kind][:, cc, :], ALU.add, [sc, vpen[kind]], [sc])
            ts("dve", B_["hi"][:], B_["B"][:], 1.0, None, ALU.add, None, [B_["B"]], [B_["hi"]])
            ts("dve", B_["lo"][:], B_["hi"][:], -1.0, None, ALU.mult, None, [B_["hi"]], [B_["lo"]])
            ts("dve", B_["d"][:], B_["hi"][:], 2.0, None, ALU.mult, None, [B_["hi"]], [B_["d"]])
            nj = (Stot + 2047) // 2048
            for it in range(NBIS):
                ck = 2.0 ** -(it + 1)
                ts("dve", B_["mid"][:], B_["d"][:], ck, B_["lo"][:, 0:1], ALU.mult, ALU.add, [B_["d"], B_["lo"]], [B_["mid"]])
                for j in range(nj):
                    w_ = min(2048, Stot - j * 2048)
                    op("dve", lambda e, j=j, w_=w_: e.tensor_scalar(out=junk[:, :w_], in0=sc[:, j * 2048:j * 2048 + w_], scalar1=B_["mid"][:, 0:1],
                                                                  scalar2=None, op0=ALU.is_ge, op1=ALU.add, accum_out=cnt4[:, j:j + 1]),
                       reads=[sc, B_["mid"]], writes=[junk, cnt4])
                if nj > 1:
                    op("dve", lambda e: e.tensor_reduce(out=B_["cnt"][:], in_=cnt4[:, :nj], axis=AX.X, op=ALU.add), reads=[cnt4], writes=[B_["cnt"]])
                    cn = B_["cnt"]
                else:
                    cn = cnt4
                ts("dve", B_["cond"][:], cn[:, 0:1], float(TOPK) - 0.5, ck, ALU.is_ge, ALU.mult, [cn], [B_["cond"]])
                stt(B_["lo"][:], B_["cond"][:], B_["d"][:, 0:1], B_["lo"][:], ALU.mult, ALU.add, [B_["cond"], B_["d"], B_["lo"]], [B_["lo"]])
                pump(3)
            flush()
            tau = B_["lo"]

            if SUB < 3:
                return
            OB = [PB[4], PB[5], PB[6]]
            for b3 in range(3):
                mm(OB[b3][:, :], zeros[:, 0:128], zeros[:, 0:512], True, False, [zeros], [OB[b3]])

            def ocol(h):
                return (h // 7), (h % 7) * 65
            tbl = 0 if kind == 0 else 1
            units = []
            for si, sg in enumerate(segs):
                for t0 in range(0, Stot, 512):
                    tw = min(512, Stot - t0)
                    for ch in range(tw // 128):
                        for half in range(2):
                            units.append((si, sg, t0, tw, ch, half))
            tstate = {}

            def s1(u):
                si, sg, t0, tw, ch, half = units[u]
                nch = tw // 128
                if ch == 0 and half == 0:
                    i2 = ti[0] % 2
                    ti[0] += 1
                    kt_, vt_, mb_ = ktt[i2], vtt[i2], mbt[i2]
                    dma(kt_[:, :, :tw], sg["KT"][:, :, t0:t0 + tw], reads=[sg["KT"]], writes=[kt_])
                    dma(vt_[:, :nch, :], sg["V"][t0:t0 + tw, :].rearrange("(c p) d -> p c d", p=128), reads=[sg["V"]], writes=[vt_])
                    ts("dve", mb_[:, :tw], sc[:, t0:t0 + tw], tau[:, 0:1], None, ALU.is_ge, None, [sc, tau], [mb_])
                    tstate[(si, t0)] = (kt_, vt_, mb_)
                kt_, vt_, mb_ = tstate[(si, t0)]
                c = t0 // 128 + ch
                near = c >= nkc - (5 if kind == 0 else 2)
                btv = bt[:].rearrange("p h q -> p (h q)")[:, 0:16 * n]
                bth = btH[half]
                if near:
                    ccp = c - (nkc - 5) if kind == 0 else c - (nkc - 2)
                    bi = ccp if kind == 0 else 5 + ccp
                    dma(btv[:, 8 * half * n:(8 * half + 8) * n], BTs[bi, :, 8 * half * n:(8 * half + 8) * n], reads=[BTs], writes=[bth, bt])
                stp = (PB[0], PB[1]) if u % 2 == 0 else (PB[2], PB[3])
                if half == 0:
                    slot = (u // 2) % 8
                    tr(PTR[:, slot * 128:(slot + 1) * 128], mb_[:, ch * 128:(ch + 1) * 128], [mb_], [PTRS[slot]])
                    cp("act", mTs[(u // 2) % 2][:, :], PTR[:, slot * 128:(slot + 1) * 128], [PTRS[slot]], [mTs[(u // 2) % 2]])
                for k2 in range(2):
                    kv = half * 2 + k2
                    pb = stp[k2]
                    outv = pb[:, :4 * n]
                    mm(outv, kt_[:, kv, ch * 128:(ch + 1) * 128], QT[:, si * 16 * n + 4 * kv * n:si * 16 * n + (4 * kv + 4) * n], True, not near, [kt_, QT], [pb])
                    if near:
                        mm(outv, antiI[:], btv[:, 4 * kv * n:(4 * kv + 4) * n], False, True, [antiI, bth], [pb])

            def s23(u):
                si, sg, t0, tw, ch, half = units[u]
                nch = tw // 128
                rs = slice(si * n, (si + 1) * n)
                kt_, vt_, mb_ = tstate[(si, t0)]
                stp = (PB[0], PB[1]) if u % 2 == 0 else (PB[2], PB[3])
                pt = PT[u % 2]
                for k2 in range(2):
                    act(pt[:, 4 * k2:4 * k2 + 4, 0:n], stp[k2][:, :4 * n].rearrange("p (h q) -> p h q", q=n), AF.Exp, [stp[k2]], [pt])
                slot = (u // 2) % 8
                mTb = mTs[(u // 2) % 2]
                mT = mTb[:, si * n:(si + 1) * n]
                tt("dve", pt[:, :, 0:n], pt[:, :, 0:n], bcast(mT.unsqueeze(1), [128, 8, n]), ALU.mult, [pt, mTb], [pt])
                for hh in range(8):
                    h = half * 8 + hh
                    kv = h // 4
                    b3, c0 = ocol(h)
                    last = (t0 + tw >= Stot) and ch == nch - 1
                    mm(OB[b3][rs, c0:c0 + 65], pt[:, hh, 0:n], vt_[:, ch, kv * 65:(kv + 1) * 65], False, last, [pt, vt_], [OB[b3]])

            s1(0)
            for u in range(len(units)):
                if u + 1 < len(units):
                    s1(u + 1)
                s23(u)
            for b3 in range(3):
                nh = 7 if b3 < 2 else 2
                ov = OB[b3][:, 0:nh * 65].rearrange("p (h d) -> p h d", d=65)
                op("dve", lambda e, ov=ov, b3=b3, nh=nh: e.reciprocal(out=rinv[:, b3 * 7:b3 * 7 + nh].unsqueeze(2), in_=ov[:, :, 64:65]),
                   reads=[OB[b3]], writes=[rinv])
                tt("dve", osb[:, b3 * 7 * 64:(b3 * 7 + nh) * 64].rearrange("p (h d) -> p h d", d=64), ov[:, :, 0:64],
                   bcast(rinv[:, b3 * 7:b3 * 7 + nh].unsqueeze(2), [128, nh, 64]), ALU.mult, [OB[b3], rinv], [osb])

            def transpose8(src, dst):
                for j in range(8):
                    tr(PTR[:, j * 128:(j + 1) * 128], src[:, j * 128:(j + 1) * 128], [src], PTRS)
                cp("act", dst[:].rearrange("p j q -> p (j q)"), PTR[:, :], PTRS, [dst])
            if SUB < 4:
                return
            transpose8(osb, tT)
            for hf in range(2):
                def ev_y2(pb, hf=hf):
                    tt("dve", hs[:, hf * 512:(hf + 1) * 512], pb[:, :],
                       gate[:, 1024 + hf * 512:1024 + (hf + 1) * 512], ALU.mult, [pb, gate], [hs])
                tokmm(w_att, hf * 512, 512, lambda kc: tT[:, kc, :], 8, ev_y2, [tT])
            tt("dve", mb16[:], m1[:], hs[:], ALU.add, [m1, hs], [mb16])
            transpose8(mb16, tT)
            dma(xrow[:], segs[0]["xrow"], writes=[xrow])

            def layer_norm(src, gi, dst, dstap, lnt):
                op("dve", lambda e: e.tensor_reduce(out=st4[:, 0:1], in_=src[:], axis=AX.X, op=ALU.add), reads=[src], writes=[st4])
                ts("dve", st4[:, 1:2], st4[:, 0:1], -1.0 / D, None, ALU.mult, None, [st4], [st4])
                ts("dve", src[:], src[:], st4[:, 1:2], None, ALU.add, None, [src, st4], [src])
                act(yo[:, 0:D], src[:], AF.Square, [src], [yo, st4], accum_out=st4[:, 2:3])
                act(st4[:, 3:4], st4[:, 2:3], AF.Sqrt, [st4], [st4], scale=1.0 / D, bias=1e-5)
                op("dve", lambda e: e.reciprocal(out=st4[:, 4:5], in_=st4[:, 3:4]), reads=[st4], writes=[st4])
                ts("dve", src[:], src[:], st4[:, 4:5], None, ALU.mult, None, [src, st4], [src])
                dma(lnt[:], lnp[gi], writes=[lnt])
                tt("dve", src[:], src[:], lnt[:], ALU.mult, [src, lnt], [src])
                dma(lnt[:], lnp[gi + 1], writes=[lnt])
                tt("dve", dstap, src[:], lnt[:], ALU.add, [src, lnt], [dst])

            for hf in range(2):
                def ev_mix(pb, hf=hf):
                    stt(m1[:, hf * 512:(hf + 1) * 512], xrow[:, hf * 512:(hf + 1) * 512], ALPHA, pb[:, :], ALU.mult, ALU.add, [xrow, pb], [m1])
                tokmm(w_o, hf * 512, 512, lambda kc: tT[:, kc, :], 8, ev_mix, [tT])
            layer_norm(m1, 0, hs, hs[:], View(ysb, ysb[:, 0:D]))
            cp("pool", mb16[:], hs[:], [hs], [mb16])
            transpose8(mb16, tT)
            for c4 in range(8):
                w = nextw()
                wload(w, w_up, 0, c4 * 512, 512)
                pb = bank()
                for j in range(4):
                    for kc in range(8):
                        mm(pb[:, j * 128:(j + 1) * 128], w[:, kc, j * 128:(j + 1) * 128], tT[:, kc, :], kc == 0, kc == 7, [w, tT], [pb])
                r_ = rl[c4 % 2]
                act(r_[:, :], pb[:, :], AF.Relu, [pb], [r_])
                tt("pool", fT[:, c4 * 4:(c4 + 1) * 4, :].rearrange("p j q -> p (j q)"), r_[:, :], r_[:, :], ALU.mult, [r_], [fT])
            for hf in range(2):
                def ev_dn(pb, hf=hf):
                    stt(m1[:, hf * 512:(hf + 1) * 512], hs[:, hf * 512:(hf + 1) * 512], ALPHA, pb[:, :], ALU.mult, ALU.add, [hs, pb], [m1])
                tokmm(w_dn, hf * 512, 512, lambda kc: fT[:, kc, :], 32, ev_dn, [fT])
            layer_norm(m1, 2, ysb, ysb[:, 0:D], View(yo, yo[:, 0:D]))
            dma(outs["y"], ysb[:, 0:D], reads=[ysb], writes=[outs["ybuf"]])

        op("pool", lambda e: e.memset(H[:], 0.0), writes=[H])
        pseg = dict(KT=KTp, V=Vp, kiT=kiTp)
        import os
        STAGE = int(os.environ.get("K_STAGE", "3"))
        def state_jobs(k):
            for i in range(4):
                blk = 4 * k + i
                sg = dict(pseg, wpos=blk * 128)
                outs = dict(k=k_p, v=v_p, ki=ki_p, row0=blk * 128)
                if blk == NB - 1:
                    outs["conv"] = conv_p
                yield from job(0, False, xT_p[:, :, blk * 128:blk * 128 + 131], [sg], blk_i=i, outs=outs)
                yield

        BGOV = os.environ.get("K_BG", "1") == "1"
        if STAGE >= 1:
            run(state_jobs(0))
        for k in range(NSLOT if STAGE >= 1 else 0):
            if k + 1 < NSLOT:
                if BGOV and STAGE >= 2:
                    bg[0] = state_jobs(k + 1)
            sg = dict(pseg, wpos=None, nkc=4 * k + 4, xrow=xrow_p[k])
            if STAGE >= 2:
                run(job(0, True, xTo_p[k], [sg], outs=dict(y=y_p[k], ybuf=y_p)))
                flush()
            if k + 1 < NSLOT and not (BGOV and STAGE >= 2):
                run(state_jobs(k + 1))
        dma(ssm_p[:], H[:], reads=[H], writes=[ssm_p])
        for jj in range(2 if STAGE >= 3 else 0):
            ssegs = [dict(KT=KTs[2 * jj + i], V=Vs[2 * jj + i], kiT=kiTs[2 * jj + i], wpos=PAST, nkc=NKC_S, xrow=xrow_s[jj]) for i in range(2)]
            run(job(1, True, xT_s[jj], ssegs, outs=dict(y=y_s[jj], ybuf=y_s, k=k_s, v=v_s, ki=ki_s, row0=128 * jj, conv=View(conv_s, conv_s[jj])),
                    aux=dict(convst=convst_s[jj], ssm_in=[ssmT_s[2 * jj + i] for i in range(2)], ssm_out=[ssm_s[2 * jj + i] for i in range(2)])))
        fw.finish()
    return nc


def _t5_bucket(rel):
    rel = np.asarray(rel, np.int64)
    half, max_exact = 16, 8
    n = np.abs(rel)
    lg = np.log(np.maximum(n, max_exact).astype(np.float32) / np.float32(max_exact)) / np.float32(math.log(128 / max_exact)) * np.float32(half - max_exact)
    large = max_exact + lg.astype(np.float32).astype(np.int32)
    large = np.minimum(large, half - 1)
    return np.where(rel > 0, half, 0) + np.where(n < max_exact, n, large)


_CACHE = {}


def kernel(x_prompt, x_sample, cache_k, cache_v, cache_kidx, state_ssm, state_conv, rel_bias,
           w_in, conv_w, conv_b, dt_bias, a_log, d_skip, ssd_norm_w, w_ssd_o, w_attn_o, w_out,
           ln1_g, ln1_b, w_up, w_down, ln2_g, ln2_b):
    f = lambda a: np.ascontiguousarray(np.asarray(a, dtype=np.float32))
    x_prompt, x_sample = f(x_prompt), f(x_sample)
    BATCH, SEQ, _ = x_prompt.shape
    DECB, DSEQ, _ = x_sample.shape
    PAST = cache_k.shape[2]
    assert BATCH == 2 and DECB == 32 and DSEQ == 32
    NB = SEQ // 128
    NSLOT = NB // 4
    key = (SEQ, PAST)
    if key not in _CACHE:
        _CACHE[key] = build(SEQ, PAST)
    nc = _CACHE[key]
    cache_k, cache_v, cache_kidx, state_ssm, state_conv = f(cache_k), f(cache_v), f(cache_kidx), f(state_ssm), f(state_conv)
    shared = {
        "w_in": f(w_in[0]), "w_ssd": f(w_ssd_o[0]), "w_att": f(w_attn_o[0]), "w_o": f(w_out[0]),
        "w_up": f(w_up[0]), "w_dn": f(w_down[0]),
        "convw": f(np.asarray(conv_w[0]).reshape(4, 24, 128).transpose(2, 1, 0)),
        "convb": f(np.asarray(conv_b[0]).reshape(24, 128).T),
        "dtb": f(np.broadcast_to(np.asarray(dt_bias[0])[None, :], (128, 32))),
        "alog": f(np.broadcast_to(np.asarray(a_log[0])[None, :], (128, 32))),
        "dsk": f(np.broadcast_to(np.asarray(d_skip[0])[None, :], (128, 32))),
        "normw": f(np.asarray(ssd_norm_w[0]).reshape(16, 128).T),
        "lnp": f(np.stack([np.broadcast_to(np.asarray(a[0])[None, :], (128, D)) for a in (ln1_g, ln1_b, ln2_g, ln2_b)])),
        "relb": f(rel_bias),
    }
    ar = np.arange(128)
    cm = np.zeros((2, 3, 128, 128), np.float32)
    cm[0, 0] = (ar[:, None] <= ar[None, :])
    cm[0, 1] = 1.0
    cm[0, 2] = np.where(ar[:, None] <= ar[None, :], 0.0, -1e4)
    same = (ar[:, None] // 64) == (ar[None, :] // 64)
    cm[1, 0] = same & (ar[:, None] <= ar[None, :])
    cm[1, 1] = same
    cm[1, 2] = np.where(same & (ar[:, None] <= ar[None, :]), 0.0, -1e4)
    idr = np.zeros((3, 128, 4, 128), np.float32)
    idr[0] = (ar[:, None, None] == ar[None, None, :])
    for si in range(2):
        idr[1 + si, :, :, :64] = (((ar[:, None, None] % 64) == np.arange(64)[None, None, :]) & ((ar[:, None, None] // 64) == si))
    shared["cmats"] = cm
    shared["idrep"] = idr
    xT = [np.concatenate([np.zeros((128, KC, 3), np.float32), x_prompt[b].T.reshape(KC, 128, SEQ).transpose(1, 0, 2)], axis=2) for b in range(2)]
    in_maps = []
    for c in range(8):
        b, r = c // 4, c % 4
        m = dict(shared)
        m["xT_p"] = np.ascontiguousarray(xT[b])
        own = [4 * k + r for k in range(NSLOT)]
        m["xrow_p"] = np.ascontiguousarray(np.stack([x_prompt[b, j * 128:(j + 1) * 128] for j in own]))
        m["xTo_p"] = np.ascontiguousarray(np.stack([xT[b][:, :, j * 128:j * 128 + 131] for j in own]))
        xs = x_sample[4 * c:4 * c + 4]
        xts = np.zeros((2, 128, KC, 2, 67), np.float32)
        xts[:, :, :, :, 3:35] = xs.transpose(2, 0, 1).reshape(KC, 128, 2, 2, 32).transpose(2, 1, 0, 3, 4)
        m["xT_s"] = xts.reshape(2, 128, KC, 134)
        xr = np.zeros((2, 2, 64, D), np.float32)
        xr[:, :, :32] = xs.reshape(2, 2, 32, D)
        m["xrow_s"] = xr.reshape(2, 128, D)
        sc_ = state_conv[0, 4 * c:4 * c + 4]
        m["convst_s"] = np.ascontiguousarray(sc_.reshape(2, 2, 3, 24, 128).transpose(0, 4, 3, 1, 2))
        m["rowvalid"] = ((ar % 64) < 32).astype(np.float32).reshape(128, 1)
        m["ssmT_s"] = np.ascontiguousarray(state_ssm[0, 4 * c:4 * c + 4].transpose(0, 3, 1, 2).reshape(4, 128, 2048))
        m["ckT_s"] = np.ascontiguousarray(cache_k[0, 4 * c:4 * c + 4].transpose(0, 3, 2, 1))
        m["cv_s"] = np.ascontiguousarray(cache_v[0, 4 * c:4 * c + 4].reshape(4, PAST, 256))
        m["ckiT_s"] = np.ascontiguousarray(cache_kidx[0, 4 * c:4 * c + 4].transpose(0, 2, 1))
        oh = np.zeros((2, 32, NREL), np.float32)
        ii = np.arange(NREL)
        for t, rr in enumerate((r, 0)):
            bk = _t5_bucket(128 * (3 - rr) + 127 - ii)
            oh[t, bk, ii] = 1.0
        m["onehot"] = oh
        vp = np.zeros((2, 128, 4, 128), np.float32)
        qpos = 128 * r + ar
        vend = (qpos // 64 + 1) * 64
        kpos = 128 * np.arange(4)[None, :, None] + ar[None, None, :]
        vp[0] = np.where(kpos < vend[:, None, None], 0.0, -1e30)
        vp[1, :, 0, 32:] = -1e30
        m["vispen"] = vp
        o4 = np.zeros((128, 4), np.float32)
        o4[:, r] = 1.0
        m["oh4"] = o4
        in_maps.append(m)
    res = run_bass_kernel_spmd(nc, in_maps, core_ids=list(range(8))).results
    y_prompt = np.zeros((2, SEQ, D), np.float32)
    y_sample = np.zeros((32, 32, D), np.float32)
    k_pr = np.zeros((1, 2, SEQ, 4, 64), np.float32)
    v_pr = np.zeros((1, 2, SEQ, 4, 64), np.float32)
    ki_pr = np.zeros((1, 2, SEQ, 64), np.float32)
    ssm_pr = np.zeros((1, 2, 32, 64, 128), np.float32)
    conv_pr = np.zeros((1, 2, 3, 3072), np.float32)
    k_sm = np.zeros((1, 32, 32, 4, 64), np.float32)
    v_sm = np.zeros((1, 32, 32, 4, 64), np.float32)
    ki_sm = np.zeros((1, 32, 32, 64), np.float32)
    ssm_sm = np.zeros((1, 32, 32, 64, 128), np.float32)
    conv_sm = np.zeros((1, 32, 3, 3072), np.float32)
    for c in range(8):
        b, r = c // 4, c % 4
        o = res[c]
        for k in range(NSLOT):
            j = 4 * k + r
            y_prompt[b, j * 128:(j + 1) * 128] = o["y_p"][k]
        y_sample[4 * c:4 * c + 4] = o["y_s"].reshape(4, 64, D)[:, :32]
        if r == 0:
            k_pr[0, b] = o["k_p"].reshape(SEQ, 4, 64)
            v_pr[0, b] = o["v_p"].reshape(SEQ, 4, 64)
            ki_pr[0, b] = o["ki_p"]
            ssm_pr[0, b] = o["ssm_p"].reshape(128, 32, 64).transpose(1, 2, 0)
            conv_pr[0, b] = o["conv_p"].transpose(2, 1, 0).reshape(3, 3072)
        k_sm[0, 4 * c:4 * c + 4] = o["k_s"].reshape(4, 64, 4, 64)[:, :32]
        v_sm[0, 4 * c:4 * c + 4] = o["v_s"].reshape(4, 64, 4, 64)[:, :32]
        ki_sm[0, 4 * c:4 * c + 4] = o["ki_s"].reshape(4, 64, 64)[:, :32]
        ssm_sm[0, 4 * c:4 * c + 4] = o["ssm_s"].reshape(4, 128, 32, 64).transpose(0, 2, 3, 1)
        conv_sm[0, 4 * c:4 * c + 4] = o["conv_s"].transpose(0, 3, 4, 2, 1).reshape(4, 3, 3072)
    return (y_prompt, y_sample, k_pr, v_pr, ki_pr, ssm_pr, conv_pr, k_sm, v_sm, ki_sm, ssm_sm, conv_sm)
```
